# Optimizing a Trainium2 kernel written in Bass

```python
import jax, jax.numpy as jnp
from jax import lax
import numpy as np

D_MODEL = 2048
BATCH = 4
SEQ = 2048
DEPTH = 1
DEC_BATCH = 32
DEC_SEQ = 1
PAST_LEN = 16384
PAGE_SIZE = 128

MIX_WIDTH = D_MODEL
ATTN_WIDTH = MIX_WIDTH // 2
CONV_CH = MIX_WIDTH - ATTN_WIDTH
HEAD_DIM = 64
N_HEADS = ATTN_WIDTH // HEAD_DIM
N_KV_HEADS = 4
GROUP = N_HEADS // N_KV_HEADS
KV_DIM = N_KV_HEADS * HEAD_DIM
WINDOW = 128
CONV_WIDTH = 31
D_FF = 4 * D_MODEL
ROPE_THETA = 10000.0
EPS = 1e-6
IN_COLS = ATTN_WIDTH + 2 * KV_DIM + 2 * CONV_CH
NEG = -1e30

kernel_name = "hymba_conformer_swa_sink_decoder_step"


def rms_norm(x, g):
    xf = x.astype(jnp.float32)
    y = xf * lax.rsqrt(jnp.mean(xf * xf, axis=-1, keepdims=True) + EPS)
    return (y * g.astype(jnp.float32)).astype(x.dtype)


def layer_norm(x, g, b):
    xf = x.astype(jnp.float32)
    mu = jnp.mean(xf, axis=-1, keepdims=True)
    xc = xf - mu
    y = xc * lax.rsqrt(jnp.mean(xc * xc, axis=-1, keepdims=True) + EPS)
    return (y * g.astype(jnp.float32) + b.astype(jnp.float32)).astype(x.dtype)


def rope(x, pos):
    half = HEAD_DIM // 2
    inv = ROPE_THETA ** (-jnp.arange(half, dtype=jnp.float32) / half)
    ang = pos.astype(jnp.float32)[:, None] * inv[None, :]
    cos = jnp.cos(ang)[:, None, :]
    sin = jnp.sin(ang)[:, None, :]
    xf = x.astype(jnp.float32)
    x1, x2 = xf[..., :half], xf[..., half:]
    return jnp.concatenate([x1 * cos - x2 * sin, x2 * cos + x1 * sin], axis=-1).astype(x.dtype)


def softmax_with_sink(s, sinks, mask):
    sink = sinks.astype(jnp.float32).reshape(N_KV_HEADS, GROUP, 1, 1)
    s = jnp.where(mask, s, NEG)
    m = jnp.maximum(jnp.max(s, axis=-1, keepdims=True), sink)
    p = jnp.exp(s - m)
    return p / (jnp.sum(p, axis=-1, keepdims=True) + jnp.exp(sink - m))


def attn_prompt(q, k, v, sinks):
    B, S = q.shape[0], q.shape[1]
    NB = S // WINDOW
    qb = q.reshape(B, NB, WINDOW, N_KV_HEADS, GROUP, HEAD_DIM)
    kb = k.reshape(B, NB, WINDOW, N_KV_HEADS, HEAD_DIM)
    vb = v.reshape(B, NB, WINDOW, N_KV_HEADS, HEAD_DIM)
    pad = ((0, 0), (1, 0), (0, 0), (0, 0), (0, 0))
    kk = jnp.concatenate([jnp.pad(kb, pad)[:, :-1], kb], axis=2)
    vv = jnp.concatenate([jnp.pad(vb, pad)[:, :-1], vb], axis=2)
    s = jnp.einsum('bnqkgd,bnskd->bnkgqs', qb, kk,
                   preferred_element_type=jnp.float32) * (HEAD_DIM ** -0.5)
    qi = jnp.arange(WINDOW)[:, None] + WINDOW
    kj = jnp.arange(2 * WINDOW)[None, :]
    diff = qi - kj
    band = (diff >= 0) & (diff < WINDOW)
    blk = jnp.arange(NB)[:, None, None]
    valid = (blk > 0) | (kj[None] >= WINDOW)
    mask = (band[None] & valid)[None, :, None, None]
    p = softmax_with_sink(s, sinks, mask)
    o = jnp.einsum('bnkgqs,bnskd->bnqkgd', p.astype(v.dtype), vv)
    return o.reshape(B, S, N_HEADS * HEAD_DIM)


def attn_sample(q, k, v, k_buf, v_buf, sinks):
    B, DS = q.shape[0], q.shape[1]
    WB = k_buf.shape[1]
    kk = jnp.concatenate([k_buf, k], axis=1)
    vv = jnp.concatenate([v_buf, v], axis=1)
    qg = q.reshape(B, DS, N_KV_HEADS, GROUP, HEAD_DIM)
    s = jnp.einsum('bqkgd,bskd->bkgqs', qg, kk,
                   preferred_element_type=jnp.float32) * (HEAD_DIM ** -0.5)
    q_pos = PAST_LEN + jnp.arange(DS)
    k_pos = PAST_LEN - WB + jnp.arange(WB + DS)
    diff = q_pos[:, None] - k_pos[None, :]
    mask = (diff >= 0) & (diff < WINDOW)
    p = softmax_with_sink(s, sinks, mask)
    o = jnp.einsum('bkgqs,bskd->bqkgd', p.astype(v.dtype), vv)
    return o.reshape(B, DS, N_HEADS * HEAD_DIM), kk[:, -WB:], vv[:, -WB:]


def causal_dwconv(u, buf, w, b):
    full = jnp.concatenate([buf, u], axis=1)
    y = lax.conv_general_dilated(full, w[:, None, :], window_strides=(1,), padding='VALID',
                                 dimension_numbers=('NWC', 'WIO', 'NWC'),
                                 feature_group_count=CONV_CH)
    return y + b, full[:, -(CONV_WIDTH - 1):]


def layer(x, pos, k_buf, v_buf, c_buf, norm_mix_g, w_in, q_norm_g, k_norm_g, sinks,
          conv_w, conv_b, conv_ln_g, conv_ln_b, w_out, norm_mlp_g, w_up, w_down, is_prompt):
    B, S = x.shape[0], x.shape[1]
    h = rms_norm(x, norm_mix_g)
    z = h @ w_in
    q = z[..., :ATTN_WIDTH].reshape(B, S, N_HEADS, HEAD_DIM)
    k = z[..., ATTN_WIDTH:ATTN_WIDTH + KV_DIM].reshape(B, S, N_KV_HEADS, HEAD_DIM)
    v = z[..., ATTN_WIDTH + KV_DIM:ATTN_WIDTH + 2 * KV_DIM].reshape(B, S, N_KV_HEADS, HEAD_DIM)
    u_val = z[..., ATTN_WIDTH + 2 * KV_DIM:ATTN_WIDTH + 2 * KV_DIM + CONV_CH]
    u_gate = z[..., ATTN_WIDTH + 2 * KV_DIM + CONV_CH:]
    q = rope(rms_norm(q, q_norm_g), pos)
    k = rope(rms_norm(k, k_norm_g), pos)
    if is_prompt:
        a = attn_prompt(q, k, v, sinks)
        wb = min(WINDOW, S)
        new_k, new_v = k[:, -wb:], v[:, -wb:]
    else:
        a, new_k, new_v = attn_sample(q, k, v, k_buf, v_buf, sinks)
    u = u_val * jax.nn.sigmoid(u_gate)
    c, new_c = causal_dwconv(u, c_buf, conv_w, conv_b)
    c = jax.nn.silu(layer_norm(c, conv_ln_g, conv_ln_b))
    x = x + jnp.concatenate([a, c], axis=-1) @ w_out
    hm = rms_norm(x, norm_mlp_g)
    x = x + jnp.square(jax.nn.relu(hm @ w_up)) @ w_down
    return x, new_k, new_v, new_c


def setup_inputs(seed: int = 0) -> dict:
    key = jax.random.key(seed)
    ks = jax.random.split(key, 20)
    f32 = jnp.float32
    wb = min(WINDOW, PAST_LEN)
    nrm = lambda k, shp, s: jax.random.normal(k, shp, f32) * s
    return {
        "x_prompt": nrm(ks[0], (BATCH, SEQ, D_MODEL), 1.0),
        "x_sample": nrm(ks[1], (DEC_BATCH, DEC_SEQ, D_MODEL), 1.0),
        "cache_k": nrm(ks[2], (DEPTH, DEC_BATCH, wb, N_KV_HEADS, HEAD_DIM), 1.0),
        "cache_v": nrm(ks[3], (DEPTH, DEC_BATCH, wb, N_KV_HEADS, HEAD_DIM), 1.0),
        "state_conv": nrm(ks[4], (DEPTH, DEC_BATCH, CONV_WIDTH - 1, CONV_CH), 0.5),
        "norm_mix_g": 1.0 + nrm(ks[5], (DEPTH, D_MODEL), 0.02),
        "w_in": nrm(ks[6], (DEPTH, D_MODEL, IN_COLS), D_MODEL ** -0.5),
        "q_norm_g": 1.0 + nrm(ks[7], (DEPTH, HEAD_DIM), 0.02),
        "k_norm_g": 1.0 + nrm(ks[8], (DEPTH, HEAD_DIM), 0.02),
        "sinks": nrm(ks[9], (DEPTH, N_HEADS), 0.5),
        "conv_w": nrm(ks[10], (DEPTH, CONV_WIDTH, CONV_CH), CONV_WIDTH ** -0.5),
        "conv_b": nrm(ks[11], (DEPTH, CONV_CH), 0.02),
        "conv_ln_g": 1.0 + nrm(ks[12], (DEPTH, CONV_CH), 0.02),
        "conv_ln_b": nrm(ks[13], (DEPTH, CONV_CH), 0.02),
        "w_out": nrm(ks[14], (DEPTH, MIX_WIDTH, D_MODEL), MIX_WIDTH ** -0.5),
        "norm_mlp_g": 1.0 + nrm(ks[15], (DEPTH, D_MODEL), 0.02),
        "w_up": nrm(ks[16], (DEPTH, D_MODEL, D_FF), D_MODEL ** -0.5),
        "w_down": nrm(ks[17], (DEPTH, D_FF, D_MODEL), D_FF ** -0.5),
    }


def reference(x_prompt, x_sample, cache_k, cache_v, state_conv, norm_mix_g, w_in, q_norm_g,
              k_norm_g, sinks, conv_w, conv_b, conv_ln_g, conv_ln_b, w_out, norm_mlp_g,
              w_up, w_down):
    pos_p = jnp.arange(x_prompt.shape[1])
    pos_s = PAST_LEN + jnp.arange(x_sample.shape[1])
    zero_conv = jnp.zeros((x_prompt.shape[0], CONV_WIDTH - 1, CONV_CH), x_prompt.dtype)
    xp, xs = x_prompt, x_sample
    kp, vp, cp, ksm, vsm, csm = [], [], [], [], [], []
    for l in range(DEPTH):
        w = (norm_mix_g[l], w_in[l], q_norm_g[l], k_norm_g[l], sinks[l], conv_w[l], conv_b[l],
             conv_ln_g[l], conv_ln_b[l], w_out[l], norm_mlp_g[l], w_up[l], w_down[l])
        xp, nk, nv, nc = layer(xp, pos_p, None, None, zero_conv, *w, is_prompt=True)
        kp.append(nk); vp.append(nv); cp.append(nc)
        xs, nk, nv, nc = layer(xs, pos_s, cache_k[l], cache_v[l], state_conv[l], *w,
                               is_prompt=False)
        ksm.append(nk); vsm.append(nv); csm.append(nc)
    new_k_prompt = jnp.stack(kp)
    new_v_prompt = jnp.stack(vp)
    new_conv_prompt = jnp.stack(cp)
    new_k_sample = jnp.stack(ksm)
    new_v_sample = jnp.stack(vsm)
    new_conv_sample = jnp.stack(csm)
    return (xp, xs, new_k_prompt, new_v_prompt, new_conv_prompt, new_k_sample, new_v_sample, new_conv_sample)
```

```python
import bisect
import numpy as np
import ml_dtypes
import concourse.bass as bass
import concourse.mybir as mybir
from concourse.bass_utils import run_bass_kernel_spmd

F32 = mybir.dt.float32
BF16 = mybir.dt.bfloat16
U8 = mybir.dt.uint8
AF = mybir.ActivationFunctionType
ALU = mybir.AluOpType
AX = mybir.AxisListType

D = 2048
NCH = 16
SEQ = 2048
NB = 4
NS = 32
DFF = 8192
NFF = 64
EPS = 1e-6
PAST = 16384
NCORE = 8
SPC = 4

NCOL = 1158
MC0 = 129
NM = 1029
SC0 = 1153
PF = [(0, 386), (386, 772), (772, 1158)]
PM = [(0, 343), (343, 686), (686, 1029)]
TT = [("h", 1, 128)] + [("m%d" % t, MC0 + 128 * t, 128) for t in range(8)] + [("s", SC0, 4)]
GFF = 4


def head_of(c, half):
    if c < 4:
        return c if half == 0 else 4 + c
    return 8 + (c - 4) if half == 0 else 12 + (c - 4)


class Ins:
    __slots__ = ("eng", "fn", "deps", "inc", "count", "key", "is_dma", "group", "name")

    def __init__(self, eng, fn, name=""):
        self.eng = eng
        self.fn = fn
        self.deps = set()
        self.inc = False
        self.count = 0
        self.key = None
        self.is_dma = False
        self.group = False
        self.name = name


class Space:
    def __init__(self):
        self.bp = [0, 1 << 40]
        self.st = {0: [None, []]}

    def _split(self, x):
        i = bisect.bisect_left(self.bp, x)
        if self.bp[i] == x:
            return
        prev = self.bp[i - 1]
        w, r = self.st[prev]
        self.bp.insert(i, x)
        self.st[x] = [w, list(r)]

    def segs(self, a, b):
        self._split(a)
        self._split(b)
        i = bisect.bisect_left(self.bp, a)
        while self.bp[i] < b:
            yield self.st[self.bp[i]]
            i += 1


class Ref:
    __slots__ = ("ap", "rngs")

    def __init__(self, ap, rngs=()):
        self.ap = ap
        self.rngs = list(rngs)


class Buf:
    def __init__(self, space, base_ap, off, shape, esz):
        self.space = space
        self.ap = base_ap
        self.off = off
        self.shape = tuple(shape)
        self.esz = esz

    def __call__(self, *idx, p=None):
        idx = list(idx) + [None] * (len(self.shape) - len(idx))
        sl = []
        norm = []
        for i, n in zip(idx, self.shape):
            if i is None:
                sl.append(slice(None))
                norm.append((0, n))
            elif isinstance(i, tuple):
                assert 0 <= i[0] < i[1] <= n, (i, n)
                sl.append(slice(i[0], i[1]))
                norm.append(i)
            else:
                assert 0 <= i < n, (i, n)
                sl.append(i)
                norm.append((i, i + 1))
        psl = slice(None) if p is None else slice(p[0], p[1])
        ap = self.ap[(psl,) + tuple(sl)]
        strides = []
        s = 1
        for n in reversed(self.shape):
            strides.append(s)
            s *= n
        strides = strides[::-1]
        rngs = []
        dims = len(self.shape)

        def rec(d, base):
            if d == dims - 1:
                lo, hi = norm[d]
                rngs.append([base + lo, base + hi])
                return
            lo, hi = norm[d]
            inner_full = all(norm[k] == (0, self.shape[k]) for k in range(d + 1, dims))
            if inner_full:
                rngs.append([base + lo * strides[d], base + hi * strides[d]])
                return
            for i in range(lo, hi):
                rec(d + 1, base + i * strides[d])

        rec(0, 0)
        rngs.sort()
        merged = []
        for a, b in rngs:
            if merged and merged[-1][1] >= a:
                merged[-1][1] = max(merged[-1][1], b)
            else:
                merged.append([a, b])
        out = [(self.space, self.off + a * self.esz, self.off + b * self.esz) for a, b in merged]
        return Ref(ap, out)


class Prog:
    ENGS = ["pe", "act", "dve", "pool", "sp"]

    def __init__(self):
        self.streams = {e: [] for e in self.ENGS}
        self.spaces = {"sb": Space(), "ps": Space()}
        self.finals = []
        self.keycount = {}
        self.bank_last = {}
        self.n = 0

    def add(self, eng, fn, reads=(), writes=(), key=None, group=False, final=False, name="", after=()):
        ins = Ins(eng, fn, name)
        ins.deps.update(after)
        for ref in reads:
            for (sp, a, b) in ref.rngs:
                for st in self.spaces[sp].segs(a, b):
                    if st[0] is not None:
                        ins.deps.add(st[0])
        for ref in writes:
            for (sp, a, b) in ref.rngs:
                for st in self.spaces[sp].segs(a, b):
                    if st[0] is not None:
                        ins.deps.add(st[0])
                    ins.deps.update(st[1])
        banks = set()
        for ref in list(reads) + list(writes):
            for (sp, a, b) in ref.rngs:
                if sp == "ps":
                    banks.update(range(a // 2048, (b - 1) // 2048 + 1))
        for bk in banks:
            last = self.bank_last.setdefault(bk, {})
            for e2, it in last.items():
                if e2 != eng:
                    ins.deps.add(it)
            last[eng] = ins
        ins.deps.discard(ins)
        for ref in reads:
            for (sp, a, b) in ref.rngs:
                for st in self.spaces[sp].segs(a, b):
                    st[1].append(ins)
        for ref in writes:
            for (sp, a, b) in ref.rngs:
                for st in self.spaces[sp].segs(a, b):
                    st[0] = ins
                    st[1] = []
        if key is not None:
            ins.is_dma = True
            ins.key = key
            ins.group = group
            self.keycount[key] = self.keycount.get(key, 0) + 16
            ins.count = self.keycount[key]
            ins.inc = True
        if final:
            self.finals.append(ins)
        self.streams[eng].append(ins)
        self.n += 1
        return ins

    def finish(self):
        f = Ins("sp", None, "final")
        f.deps = set(self.finals)
        self.streams["sp"].append(f)

    def emit(self, nc):
        for e in self.ENGS:
            for ins in self.streams[e]:
                keep = set()
                for d in ins.deps:
                    if d.is_dma:
                        keep.add(d)
                    elif d.eng == "pe" and ins.eng == "pe" and not ins.is_dma:
                        continue
                    else:
                        keep.add(d)
                        d.inc = True
                ins.deps = keep
        for e in self.ENGS:
            c = 0
            for ins in self.streams[e]:
                if ins.is_dma:
                    continue
                if ins.inc:
                    c += 1
                    ins.count = c
            assert c < 60000, (e, c)
        sems = {}
        for e in ["pe", "act", "dve", "pool"]:
            sems[e] = nc.alloc_semaphore("s_" + e)
        for k in self.keycount:
            sems[("k", k)] = nc.alloc_semaphore("k_" + str(k))
        streams = self.streams
        keycount = self.keycount

        def replay(ename, eng):
            waited = {}
            for ins in streams[ename]:
                need = {}
                for d in ins.deps:
                    if d.is_dma:
                        s = ("k", d.key)
                        c = keycount[d.key] if d.group else d.count
                    else:
                        s = d.eng
                        c = d.count
                    if c > need.get(s, 0):
                        need[s] = c
                for s, c in need.items():
                    if waited.get(s, 0) >= c:
                        continue
                    eng.wait_ge(sems[s], c)
                    waited[s] = c
                if ins.fn is None:
                    continue
                bi = ins.fn(eng)
                if ins.is_dma:
                    bi.then_inc(sems[("k", ins.key)], 16)
                elif ins.inc:
                    bi.then_inc(sems[ins.eng], 1)

        with nc.Block() as block:
            @block.sync
            def _(e):
                replay("sp", e)

            @block.scalar
            def _(e):
                replay("act", e)

            @block.vector
            def _(e):
                replay("dve", e)

            @block.gpsimd
            def _(e):
                replay("pool", e)

            @block.tensor
            def _(e):
                replay("pe", e)


def build_program(stop=None):
    nc = bass.Bass("TRN2", target_bir_lowering=False)
    P = Prog()
    dumps = []

    def end(*bufs):
        for (name, buf) in bufs:
            ref = buf()
            shp = [128] + list(buf.shape)
            dt = F32 if buf.esz == 4 else BF16
            d = nc.dram_tensor("dbg_" + name, shp, dt, kind="ExternalOutput").ap()
            P.add("sp", (lambda d=d, ref=ref: (lambda e: e.dma_start(out=d, in_=ref.ap)))(), reads=[ref],
                  key="dbg_" + name, final=True)
        P.finish()
        P.emit(nc)
        return nc

    def din(name, shape, dt=F32):
        return nc.dram_tensor(name, list(shape), dt, kind="ExternalInput").ap()

    def dout(name, shape, dt=F32):
        return nc.dram_tensor(name, list(shape), dt, kind="ExternalOutput").ap()

    xm = din("xm", [1024, D])
    xh = din("xh", [128, D])
    xs = din("xs", [SPC, D])
    ck = din("ck", [SPC, 128, 256])
    cv = din("cv", [SPC, 128, 256])
    scv = din("sc", [SPC, 30, 1024])
    wq_d = din("wq", [16, 128, 1024])
    wkv_d = din("wkv", [16, 128, 512])
    wu_d = din("wu", [16, 128, 2048])
    wo_d = din("wo", [16, 128, 2048])
    wup_d = din("wup", [NFF, 128, 2048])
    wdn_d = din("wdn", [NFF, 128, 2048])
    par_d = din("par", [128, 640])
    tab_d = din("tab", [128, 1280])
    cst_d = din("cst", [128, 512])

    yp = dout("yp", [1024, D])
    ys = dout("ys", [SPC, D])
    nkp = dout("nkp", [128, 256])
    nvp = dout("nvp", [128, 256])
    ncp = dout("ncp", [30, 1024])
    nks = dout("nks", [SPC, 128, 256])
    nvs = dout("nvs", [SPC, 128, 256])
    ncs = dout("ncs", [SPC, 30, 1024])

    base = (nc.sbuf_base + 31) // 32 * 32
    ARENA = (nc.sbuf_top - base) // 32 * 32
    arena = nc.alloc_sbuf_tensor_at("arena", [128, ARENA], U8, offset=base)
    psum = nc.alloc_psum_tensor("psum", [128, 8, 512], F32)

    def shaped(ap, shape):
        if len(shape) == 2:
            ap = ap.rearrange("p (a b) -> p a b", b=shape[1])
        elif len(shape) == 3:
            ap = ap.rearrange("p (a b c) -> p a b c", b=shape[1], c=shape[2])
        return ap

    def sb(off, dt, shape):
        esz = 4 if dt == F32 else 2
        n = int(np.prod(shape)) * esz
        assert off % 32 == 0 and off + n <= ARENA, (off, n, ARENA)
        ap = shaped(arena[:, off:off + n].bitcast(dt), shape)
        return Buf("sb", ap, off, shape, esz)

    def ps_f32(b0, nb, shape):
        n = int(np.prod(shape))
        assert n <= nb * 512
        ap = psum[:, b0:b0 + nb, :].rearrange("p a b -> p (a b)")[:, 0:n]
        return Buf("ps", shaped(ap, shape), b0 * 2048, shape, 4)

    def ps_bf16(b0, shape):
        n = int(np.prod(shape))
        assert n <= 1024
        ap = psum[:, b0, :].bitcast(BF16)[:, 0:n]
        return Buf("ps", shaped(ap, shape), b0 * 2048, shape, 2)

    def acc3(slot):
        b0 = 3 * slot
        return Buf("ps", psum[:, b0:b0 + 3, :], b0 * 2048, (3, 512), 4)

    def sub3(buf, idx, w):
        n = buf.shape[1]
        assert n == 3 * w
        return Buf(buf.space, buf.ap[:, idx, :].rearrange("p (a b) -> p a b", b=w),
                   buf.off + idx * n * buf.esz, (3, w), buf.esz)

    o = [0]

    def take(n):
        r = o[0]
        o[0] += (n + 31) // 32 * 32
        return r

    O_XT = take(16 * NM * 4)
    O_HT = take(16 * NCOL * 2)
    O_RA = take(33024)
    O_QT = take(8 * NM * 2)
    O_KT = take(2 * NCOL * 2)
    O_VT = take(10 * 256 * 2)
    O_U = take(8 * NCOL * 2)
    O_AT = take(8 * NM * 2)
    O_TMP = take(8320)
    O_MISC = take(6400)
    assert o[0] <= ARENA, (o[0], ARENA)

    XT = sb(O_XT, F32, (16, NM))
    HT = sb(O_HT, BF16, (16, NCOL))
    HM = sb(O_HT, BF16, (16, NM))
    WQ = sb(O_RA, BF16, (16, 1024))
    CC = sb(O_RA, F32, (8, NM))
    QT = sb(O_QT, BF16, (8, NM))
    KT = sb(O_KT, BF16, (2, NCOL))
    VT = sb(O_VT, BF16, (10, 256))
    XIN = [sb(O_QT + i * 8192, F32, (D,)) for i in range(3)]
    U = sb(O_U, BF16, (8, NCOL))
    WKV = sb(O_U, BF16, (16, 512))
    XTH = sb(O_U, F32, (16, 129))
    RSTD = sb(O_U + 8256, F32, (NCOL,))
    CT = sb(O_U, BF16, (8, NM))
    AT = sb(O_AT, BF16, (8, NM))
    GCQ = sb(O_AT, F32, (10, 64))
    GSQ = sb(O_AT + 2560, F32, (10, 64))
    GCK = sb(O_AT + 5120, F32, (10, 64))
    GSK = sb(O_AT + 7680, F32, (10, 64))
    TABIN = sb(O_AT + 10240, F32, (2, 10, 64))
    m = O_MISC
    IDF = sb(m, F32, (128,)); m += 512
    IDB = sb(m, BF16, (128,)); m += 256
    ONB = sb(m, BF16, (128,)); m += 256
    ONM = sb(m, BF16, (128,)); m += 256
    MBOWN = sb(m, BF16, (128,)); m += 256
    MBPRV = sb(m, BF16, (128,)); m += 256
    MBPR0 = sb(m, BF16, (128,)); m += 256
    PAR = sb(m, F32, (640,)); m += 2560
    ESK = sb(m, F32, (8,)); m += 32
    T_SS = sb(m, F32, (16,)); m += 64
    T_RS = sb(m, F32, (16,)); m += 64
    UF = sb(m, F32, (8, 34)); m += 1088
    VB_S = sb(m, BF16, (256,)); m += 512
    assert m <= O_MISC + 6400, m - O_MISC
    KF_L = sb(O_AT + 12288, F32, (256,))
    VF_L = sb(O_AT + 13312, F32, (256,))
    KF_S = sb(O_AT + 14336, F32, (256,))
    VF_S = sb(O_AT + 15360, F32, (256,))

    G1 = lambda c: PAR((c, c + 1))
    G2 = lambda c: PAR((16 + c, 17 + c))
    CW = lambda c: PAR((32 + 31 * c, 32 + 31 * c + 31))
    CB = lambda c: PAR((280 + c, 281 + c))
    LG = lambda c: PAR((288 + c, 289 + c))
    LB = lambda c: PAR((296 + c, 297 + c))
    SKT = PAR((304, 312))
    GQ = PAR((320, 384))
    GK = PAR((384, 448))
    GQS = PAR((448, 512))
    GKS = PAR((512, 576))

    def dma(eng, out, in_, key, group=False, final=False):
        return P.add(eng, lambda e: e.dma_start(out=out.ap, in_=in_.ap), reads=[in_], writes=[out],
                     key=key, group=group, final=final)

    def act(out, in_, func, bias=0.0, scale=1.0):
        rd = [in_]
        b = bias
        s = scale
        if isinstance(bias, Ref):
            rd.append(bias)
            b = bias.ap
        if isinstance(scale, Ref):
            rd.append(scale)
            s = scale.ap
        return P.add("act", lambda e: e.activation(out=out.ap, in_=in_.ap, func=func, bias=b, scale=s),
                     reads=rd, writes=[out])

    def tt(eng, out, in0, in1, op):
        return P.add(eng, lambda e: e.tensor_tensor(out=out.ap, in0=in0.ap, in1=in1.ap, op=op),
                     reads=[in0, in1], writes=[out])

    def stt(out, in0, sc, in1, op0, op1):
        rd = [in0, in1]
        a = sc
        if isinstance(sc, Ref):
            rd.append(sc)
            a = sc.ap
        return P.add("dve", lambda e: e.scalar_tensor_tensor(out=out.ap, in0=in0.ap, scalar=a, in1=in1.ap,
                                                             op0=op0, op1=op1), reads=rd, writes=[out])

    def cp(eng, out, in_):
        if eng == "act":
            return P.add("act", lambda e: e.copy(out=out.ap, in_=in_.ap), reads=[in_], writes=[out])
        return P.add(eng, lambda e: e.tensor_copy(out=out.ap, in_=in_.ap), reads=[in_], writes=[out])

    def recip(out, in_):
        return P.add("dve", lambda e: e.reciprocal(out=out.ap, in_=in_.ap), reads=[in_], writes=[out])

    def red(out, in_):
        return P.add("dve", lambda e: e.tensor_reduce(out=out.ap, in_=in_.ap, axis=AX.X, op=ALU.add),
                     reads=[in_], writes=[out])

    def memset(eng, out, val):
        return P.add(eng, lambda e: e.memset(out.ap, val), writes=[out])

    def mm(out, lhsT, rhs, start, stop):
        return P.add("pe", lambda e: e.matmul(out.ap, lhsT.ap, rhs.ap, start=start, stop=stop),
                     reads=[lhsT, rhs], writes=[out])

    def tr(out, in_, ident):
        return P.add("pe", lambda e: e.transpose(out.ap, in_.ap, ident.ap), reads=[in_, ident], writes=[out])

    def R(ap):
        return Ref(ap)

    def bc(ref, axis, shape):
        return Ref(ref.ap.unsqueeze(axis).broadcast_to(list(shape)), ref.rngs)

    slotrr = [0]

    def next_slot():
        s = slotrr[0] % 2
        slotrr[0] += 1
        return s

    CST = sb(O_TMP, F32, (512,))
    dma("sp", PAR(), R(par_d), "par", group=True)
    dma("sp", CST(), R(cst_d), "par", group=True)
    dma("sp", TABIN(), R(tab_d.rearrange("p (a b c) -> p a b c", a=2, b=10)), "par", group=True)
    cp("dve", IDF(), CST((0, 128)))
    cp("dve", IDB(), CST((0, 128)))
    cp("dve", MBOWN(), CST((128, 256)))
    cp("dve", MBPRV(), CST((256, 384)))
    cp("dve", MBPR0(), CST((384, 512)))
    memset("dve", ONB(), 1.0)
    memset("dve", ONM(), 1.0 / 1024.0)
    for (GC, GS, g, gs) in ((GCQ, GSQ, GQ, GQS), (GCK, GSK, GK, GKS)):
        tt("pool", GC(), TABIN(0), bc(g, 1, (128, 10, 64)), ALU.mult)
        tt("pool", GS(), TABIN(1), bc(gs, 1, (128, 10, 64)), ALU.mult)
    act(ESK(), SKT, AF.Exp)
    memset("pool", XT((0, 16), (NM - 1, NM)), 0.0)
    memset("pool", VT(9), 0.0)

    XN = [sb(O_TMP + 4096 * i, BF16, (D,)) for i in range(2)]
    memset("pool", HT((0, 16), (0, 1)), 0.0)
    memset("pool", HT((0, 16), (NCOL - 1, NCOL)), 0.0)
    bank = [0]
    xin_dmas = []

    def nbank():
        b = bank[0] % 8
        bank[0] += 1
        return b

    for ti, (tn, c0, nt) in enumerate(TT):
        pp = (0, nt)
        xin = XIN[ti % 3]
        xn = XN[ti % 2]
        src_ = xh if tn == "h" else (xs if tn == "s" else xm[(ti - 1) * 128:ti * 128, :])
        xin_dmas.append(dma("sp", xin(p=pp), R(src_), "xin%d" % (ti % 3)))
        ms = T_SS((ti, ti + 1), p=pp)
        rs = T_RS((ti, ti + 1), p=pp)
        P.add("act", (lambda xn=xn, xin=xin, ms=ms, pp=pp: (lambda e: e.activation(
            out=xn(p=pp).ap, in_=xin(p=pp).ap, func=AF.Square, accum_out=ms.ap)))(),
            reads=[xin(p=pp)], writes=[xn(p=pp), ms])
        act(rs, ms, AF.Ln, bias=EPS, scale=1.0 / D)
        act(rs, rs, AF.Exp, scale=-0.5)
        for eng_, (h0, h1) in (("dve", (0, 1024)), ("pool", (1024, D))):
            P.add(eng_, (lambda xn=xn, xin=xin, rs=rs, pp=pp, h0=h0, h1=h1: (lambda e: e.tensor_scalar(
                out=xn((h0, h1), p=pp).ap, in0=xin((h0, h1), p=pp).ap, scalar1=rs.ap, scalar2=None,
                op0=ALU.mult)))(), reads=[xin((h0, h1), p=pp), rs], writes=[xn((h0, h1), p=pp)])
        if ti == 1:
            for k4 in range(4):
                dma("pool", WKV((4 * k4, 4 * k4 + 4)), R(wkv_d[4 * k4:4 * k4 + 4].rearrange("k p n -> p k n")),
                    "wkv", group=True)
        if tn != "h":
            for c4 in range(4):
                pt = ps_f32(nbank(), 1, (4, 128))
                for j in range(4):
                    c = c4 * 4 + j
                    tr(pt(j, (0, nt)), xin((c * 128, c * 128 + 128), p=pp), IDF((0, nt), p=(0, nt)))
                cp("act" if c4 < 3 else "dve", XT((c4 * 4, c4 * 4 + 4), (c0 - MC0, c0 - MC0 + nt)), pt((0, 4), (0, nt)))
        for h8 in range(2):
            pt = ps_bf16(nbank(), (8, 128))
            for j in range(8):
                c = 8 * h8 + j
                tr(pt(j, (0, nt)), xn((c * 128, c * 128 + 128), p=pp), IDB((0, nt), p=(0, nt)))
            tt("dve", HT((8 * h8, 8 * h8 + 8), (c0, c0 + nt)), pt((0, 8), (0, nt)),
               bc(PAR((8 * h8, 8 * h8 + 8)), 2, (128, 8, nt)), ALU.mult)
    for kc in range(16):
        P.add("pool", (lambda kc=kc: (lambda e: e.dma_start(out=WQ(kc).ap, in_=wq_d[kc])))(),
              writes=[WQ(kc)], key="wq", group=True, after=xin_dmas[-3:])
    if stop == 'P2':
        return end(('ht', HT))

    def rms_stats(xparts, sqbufs, pieces, width, slot):
        accb = acc3(slot)
        for c in range(16):
            sq = sqbufs[c % 2]
            for (src, d0, n) in xparts(c):
                if c % 4 in (0, 2):
                    act(sq((d0, d0 + n)), src, AF.Square)
                else:
                    tt("dve" if c % 4 == 1 else "pool", sq((d0, d0 + n)), src, src, ALU.mult)
            for pi, (a, b) in enumerate(pieces):
                mm(accb(pi, (0, width)), ONB(), sq((a, b)), c == 0, c == 15)
        return accb

    class QKBufs:
        pass

    qb = QKBufs()
    qb.SQ = sb(O_TMP, BF16, (16, 64))
    qb.A = sb(O_TMP + 2048, F32, (16, 64))
    qb.A2 = sb(O_U + 4096, F32, (16, 64))
    qb.B = sb(O_U, F32, (16, 64))
    qb.Q = sb(O_TMP + 6144, BF16, (16, 64))
    kb_ = QKBufs()
    kb_.SQ = sb(O_TMP, BF16, (4, 64))
    kb_.B = sb(O_TMP + 1024, F32, (4, 64))
    kb_.A = sb(O_TMP + 2048, F32, (4, 64))
    kb_.C = sb(O_TMP + 3072, F32, (4, 64))
    kb_.Q = sb(O_TMP + 6144, BF16, (4, 64))

    def qk_chain(accv, nh, nt, GC, GS, tix, B_, out_bf, out_f32=None):
        pp = (0, nt)
        hs = (0, nh)
        act(B_.SQ(hs, p=pp), accv(hs, p=pp), AF.Square)
        red(T_SS(hs, p=pp), B_.SQ(hs, p=pp))
        act(T_RS(hs, p=pp), T_SS(hs, p=pp), AF.Ln, bias=EPS, scale=1.0 / 64)
        act(T_RS(hs, p=pp), T_RS(hs, p=pp), AF.Exp, scale=-0.5)
        tt("dve", B_.A(hs, p=pp), accv(hs, p=pp), bc(GC(tix, p=pp), 1, (nt, nh, 64)), ALU.mult)
        for d0, d1 in ((0, 32), (32, 64)):
            s0, s1 = (32, 64) if d0 == 0 else (0, 32)
            tt("dve", B_.B(hs, (d0, d1), p=pp), accv(hs, (s0, s1), p=pp),
               bc(GS(tix, (d0, d1), p=pp), 1, (nt, nh, 32)), ALU.mult)
        tt("pool", B_.A(hs, p=pp), B_.A(hs, p=pp), B_.B(hs, p=pp), ALU.add)
        rsb = bc(T_RS(hs, p=pp), 2, (nt, nh, 64))
        if out_f32 is not None:
            tt("pool", out_f32, B_.A(hs, p=pp), rsb, ALU.mult)
            cp("pool", out_bf, out_f32)
        else:
            tt("pool", out_bf, B_.A(hs, p=pp), rsb, ALU.mult)

    Q2 = [sb(O_TMP + 6144, BF16, (16, 64)), sb(O_U + 8192, BF16, (16, 64)), sb(O_AT + 10240, BF16, (16, 64))]
    K2 = [sb(O_TMP + 6144 + 512 * i, BF16, (4, 64)) for i in range(3)]

    def fin_qk(qbuf, nchunk, dst, ti, c0d, nt):
        pt = ps_bf16(6 + ti % 2, (8, 128))
        qf = Buf("sb", qbuf.ap.rearrange("p h d -> p (h d)"), qbuf.off, (qbuf.shape[0] * 64,), 2)
        for c in range(nchunk):
            tr(pt(c, (0, nt)), qf((c * 128, c * 128 + 128), p=(0, nt)), IDB((0, nt), p=(0, nt)))
        cp("act", dst((0, nchunk), (c0d, c0d + nt)), pt((0, nchunk), (0, nt)))

    pend = []
    for ti, (tn, c0, nt) in enumerate(TT):
        pp = (0, nt)
        slot = next_slot()
        acck = ps_f32(3 * slot, 1, (8, 64))
        for kc in range(16):
            mm(acck(p=pp), HT(kc, (c0, c0 + nt)), WKV(kc), kc == 0, kc == 15)
        if tn == "s":
            cp("act", VB_S(p=pp), acck((4, 8), p=pp))
            cp("act", VF_S(p=pp), acck((4, 8), p=pp))
        else:
            cp("act", VT(ti, p=pp), acck((4, 8), p=pp))
            if tn == "m7":
                cp("act", VF_L(), acck((4, 8)))
        kf = KF_L() if tn == "m7" else (KF_S(p=pp) if tn == "s" else kb_.C(p=pp))
        kf4 = Ref(kf.ap.rearrange("p (h d) -> p h d", d=64), kf.rngs) if tn in ("m7", "s") else kf
        kout = K2[ti % 3]
        qk_chain(acck, 4, nt, GCK, GSK, ti, kb_, kout(p=pp), out_f32=kf4)
        if len(pend) == 2:
            fin_qk(*pend.pop(0))
        pend.append((kout, 2, KT, ti, c0, nt))
    kpend = pend
    order = [2 * c + j for c in range(8) for j in (1, 0)]
    WSUF = [sb(O_U + 10240 + 4096 * i, BF16, (16, 128)) for i in range(2)]
    for i in range(2):
        dma("pool", WSUF[i](), R(wu_d[order[i]].rearrange("p (k m) -> p k m", m=128)), "wsuf%d" % i)
    pend = []
    qpend = pend
    for ti, (tn, c0, nt) in enumerate(TT):
        if tn == "h":
            continue
        pp = (0, nt)
        slot = next_slot()
        accq = ps_f32(3 * slot, 2, (16, 64))
        for kc in range(16):
            for hf in range(2):
                mm(accq((8 * hf, 8 * hf + 8), p=pp), HT(kc, (c0, c0 + nt)), WQ(kc, (512 * hf, 512 * hf + 512)),
                   kc == 0, kc == 15)
        while kpend:
            fin_qk(*kpend.pop(0))
        qbt = QKBufs()
        qbt.SQ, qbt.B = qb.SQ, qb.B
        qbt.A = qb.A if ti % 2 == 0 else qb.A2
        qout = Q2[ti % 3]
        qk_chain(accq, 16, nt, GCQ, GSQ, ti, qbt, qout(p=pp))
        if len(pend) == 2:
            fin_qk(*pend.pop(0))
        pend.append((qout, 8, QT, ti, c0 - MC0, nt))

    memset("pool", KT((0, 2), (0, 1)), 0.0)
    memset("pool", KT((0, 2), (NCOL - 1, NCOL)), 0.0)
    memset("pool", QT((0, 8), (NM - 1, NM)), 0.0)

    dma("sp", R(nkp), KF_L(), "kvout", group=True, final=True)
    dma("sp", R(nvp), VF_L(), "kvout", group=True, final=True)
    dma("sp", R(nks[:, 127, :]), KF_S(p=(0, SPC)), "kvout", group=True, final=True)
    dma("sp", R(nvs[:, 127, :]), VF_S(p=(0, SPC)), "kvout", group=True, final=True)
    dma("sp", R(nks[:, 0:127, :]), R(ck[:, 1:128, :]), "d2d", group=True, final=True)
    dma("sp", R(nvs[:, 0:127, :]), R(cv[:, 1:128, :]), "d2d", group=True, final=True)
    dma("sp", R(ncs[:, 0:29, :]), R(scv[:, 1:30, :]), "d2d", group=True, final=True)
    if stop == 'P4':
        return end(('qt', QT), ('kt', KT), ('vt', VT))

    WSU = [sb(O_RA + 12384 + 4096 * i, BF16, (16, 128)) for i in range(3)]
    wsi = [0]

    def wload(slots, src_ap, tag, m=128):
        i = wsi[0] % len(slots)
        wsi[0] += 1
        dma("pool", slots[i](), R(src_ap.rearrange("p (k m) -> p k m", m=m)), "%s%d" % (tag, i))
        return slots[i]

    def conv_tap(c, j, col0=0, ncol=NM):
        g0 = MC0 + j - 30 + col0
        uu = U(c, (g0, g0 + ncol))
        cc = CC(c, (col0, col0 + ncol))
        if j == 0:
            P.add("dve", (lambda c=c, uu=uu, cc=cc: (lambda e: e.tensor_scalar(
                out=cc.ap, in0=uu.ap, scalar1=CW(c).ap[:, 0:1], scalar2=CB(c).ap,
                op0=ALU.mult, op1=ALU.add)))(), reads=[uu, CW(c), CB(c)], writes=[cc])
        else:
            P.add("dve", (lambda c=c, uu=uu, cc=cc, j=j: (lambda e: e.scalar_tensor_tensor(
                out=cc.ap, in0=uu.ap, scalar=CW(c).ap[:, j:j + 1], in1=cc.ap,
                op0=ALU.mult, op1=ALU.add)))(), reads=[uu, CW(c), cc], writes=[cc])

    dve_taps = [(c, j) for c in (0, 1) for j in range(31)]
    DGF = sb(O_AT, BF16, (31, 128))
    PU = [(99, 452), (452, 805), (805, 1158)]
    SG = sb(O_TMP, F32, (3, 353))
    order = [2 * c + j for c in range(8) for j in (1, 0)]
    pending = [WSUF[0], WSUF[1]]
    for c in range(8):
        accs = []
        for j in range(2):
            w = pending.pop(0)
            nxt = 2 * c + j + 2
            if nxt < 16:
                pending.append(wload(WSU, wu_d[order[nxt]], "wsu"))
            accb = acc3(next_slot())
            for kc in range(16):
                for pi, (a, b) in enumerate(PU):
                    mm(accb(pi, (0, 353)), w(kc), HT(kc, (a, b)), kc == 0, kc == 15)
            accs.append(accb)
            while qpend:
                fin_qk(*qpend.pop(0))
            if j == 0:
                act(SG(), accb((0, 3), (0, 353)), AF.Sigmoid)
        ur = U(c, (99, NCOL))
        tt("dve", Ref(ur.ap.rearrange("p (a b) -> p a b", b=353), ur.rngs), accs[1]((0, 3), (0, 353)), SG(), ALU.mult)
        tt("dve", UF(c), accs[1](2, (1123 - 805, 1157 - 805)), SG(2, (1123 - 805, 1157 - 805)), ALU.mult)
        if c == 5:
            tt("pool", DGF(), bc(IDB(), 1, (128, 31, 128)), bc(CW(2), 2, (128, 31, 128)), ALU.mult)
        if c >= 1:
            for _ in range(7 if c < 7 else 99):
                if dve_taps:
                    conv_tap(*dve_taps.pop(0))
    if stop == 'P5':
        return end(('u', U), ('uf', UF))

    UTO = sb(O_TMP, F32, (8, 128))
    ptu = ps_f32(6, 2, (8, 128))
    for c in range(8):
        tr(ptu(c, p=(0, 34)), UF(c), IDF())
    cp("act", UTO(p=(0, 34)), ptu(p=(0, 34)))
    dma("sp", R(ncp), Ref(UTO.ap[0:30].rearrange("p a b -> p (a b)"), UTO().rngs), "uto", group=True, final=True)
    dma("sp", R(ncs[:, 29, :]), Ref(UTO.ap[30:34].rearrange("p a b -> p (a b)"), UTO().rngs), "uto", group=True,
        final=True)

    DG = [sb(O_HT + 7936 * i, BF16, (31, 128)) for i in range(2)]
    FS = sb(O_HT + 16384, F32, (8, SPC, 31))
    SCT = sb(O_HT + 20480, F32, (1024,))
    CSM = sb(O_HT + 24576, F32, (8, SPC))
    PRD = sb(O_HT + 24704, F32, (8, SPC, 31))
    assert not dve_taps
    dma("sp", SCT(p=(0, 120)), R(scv.rearrange("s r c -> (s r) c")), "sct", group=True)
    def sample_conv_prep():
        for c in range(8):
            pt = ps_f32(6 + c % 2, 1, (120,))
            tr(pt(), SCT((128 * c, 128 * c + 128), p=(0, 120)), IDF((0, 120), p=(0, 120)))
            pt3 = Buf("ps", pt.ap.rearrange("p (s r) -> p s r", r=30), pt.off, (SPC, 30), 4)
            cp("act", FS(c, (0, SPC), (0, 30)), pt3())
        cp("dve", Ref(FS.ap[:, :, :, 30], FS().rngs), UF((0, 8), (30, 34)))
        cwa = PAR((32, 280))
        cwb = Ref(cwa.ap.rearrange("p (c j) -> p c j", j=31).unsqueeze(2).broadcast_to([128, 8, SPC, 31]), cwa.rngs)
        tt("dve", PRD(), FS(), cwb, ALU.mult)
        red(CSM(), PRD())

    for j in range(31):
        conv_tap(2, j, 0, 172)
    for c in range(2, 8):
        if c == 2:
            dg = DGF
        else:
            dg = DG[c % 2]
            tt("pool", dg(), bc(IDB(), 1, (128, 31, 128)), bc(CW(c), 2, (128, 31, 128)), ALU.mult)
        accb = acc3(next_slot())
        for pi, (a, b) in enumerate(PM):
            a0 = 172 if (c == 2 and pi == 0) else a
            n_ = b - a0
            for j in range(31):
                g0 = MC0 + a0 + j - 30
                mm(accb(pi, (0, n_)), dg(j), U(c, (g0, g0 + n_)), j == 0, j == 30)
        c3 = sub3(CC, c, 343)
        if c == 2:
            act(CC(c, (172, 343)), accb(0, (0, 171)), AF.Identity, bias=CB(c))
            act(c3((1, 3)), accb((1, 3), (0, 343)), AF.Identity, bias=CB(c))
            sample_conv_prep()
        else:
            act(c3(), accb((0, 3), (0, 343)), AF.Identity, bias=CB(c))
    tt("dve", CC((0, 8), (1024, 1024 + SPC)), CSM(), bc(PAR((280, 288)), 2, (128, 8, SPC)), ALU.add)
    if stop == 'P7':
        return end(('cc', CC))

    LB16 = [sb(O_HT + 28672 + 2080 * i, BF16, (NM,)) for i in range(2)]
    LSQ = [sb(O_HT + 32832 + 2080 * i, BF16, (NM,)) for i in range(2)]
    accm = acc3(next_slot())
    accv = acc3(next_slot())
    for c in range(8):
        cb16 = LB16[c % 2]
        csq = LSQ[c % 2]
        cp("dve", cb16(), CC(c))
        if c % 3 == 2:
            tt("dve", csq(), CC(c), CC(c), ALU.mult)
        else:
            act(csq(), CC(c), AF.Square)
        for pi, (a, b) in enumerate(PM):
            mm(accm(pi, (0, 343)), ONM(), cb16((a, b)), c == 0, c == 7)
        for pi, (a, b) in enumerate(PM):
            mm(accv(pi, (0, 343)), ONM(), csq((a, b)), c == 0, c == 7)
    MEAN = sb(O_HT, F32, (3, 343))
    RSL = sb(O_HT + 4128, F32, (3, 343))
    SGL = [sb(O_HT + 8256 + 2080 * i, BF16, (3, 343)) for i in range(2)]
    cp("dve", MEAN(), accm((0, 3), (0, 343)))
    act(RSL(), accm((0, 3), (0, 343)), AF.Square)
    tt("dve", RSL(), accv((0, 3), (0, 343)), RSL(), ALU.subtract)
    P.add("dve", lambda e: e.tensor_scalar(out=RSL().ap, in0=RSL().ap, scalar1=0.0, scalar2=None, op0=ALU.max),
          reads=[RSL()], writes=[RSL()])
    act(RSL(), RSL(), AF.Ln, bias=EPS)
    act(RSL(), RSL(), AF.Exp, scale=-0.5)

    def ln_chunk_a(c):
        c3 = sub3(CC, c, 343)
        tt("pool", c3(), c3(), MEAN(), ALU.subtract)
        tt("pool", c3(), c3(), RSL(), ALU.mult)
        P.add("pool", (lambda c3=c3, c=c: (lambda e: e.tensor_scalar(
            out=c3().ap, in0=c3().ap, scalar1=LG(c).ap, scalar2=LB(c).ap, op0=ALU.mult, op1=ALU.add)))(),
            reads=[c3(), LG(c), LB(c)], writes=[c3()])

    def ln_chunk_b(c):
        c3 = sub3(CC, c, 343)
        sg = SGL[c % 2]
        act(sg(), c3(), AF.Sigmoid)
        tt("dve", sub3(CT, c, 343)(), c3(), sg(), ALU.mult)

    PT = [sb(O_TMP + 1024 * i, BF16, (4, 128)) for i in range(8)]
    REC = [sb(O_HT + 16384 + 2048 * i, F32, (4, 128)) for i in range(2)]
    units = [(n, gp) for n in range(8) for gp in range(2)]

    def unit_a(ui):
        n, gp = units[ui]
        own_c0 = MC0 + 128 * n
        cs = (gp * 4, gp * 4 + 4)
        pso = ps_f32(4 + ui % 2, 1, (4, 128))
        psd = ps_f32(6 + ui % 2, 1, (4, 128))
        pts = {}
        for kb, kc0 in enumerate((own_c0 - 128, own_c0)):
            for gi in range(2):
                pr = (64 * gi, 64 * gi + 64)
                pss = ps_f32(2 * gi + kb, 1, (4, 128))
                mb = MBOWN if kb == 1 else (MBPR0 if n == 0 else MBPRV)
                mm(pss(), KT(gp, (kc0, kc0 + 128), p=pr), QT(cs, (own_c0 - MC0, own_c0 - MC0 + 128), p=pr),
                   True, False)
                mm(pss(), IDB(), bc(mb(), 1, (128, 4, 128)), False, True)
                pt = PT[(ui % 2) * 4 + 2 * gi + kb]
                act(pt(), pss(), AF.Exp, scale=0.125)
                pts[(gi, kb)] = pt
        for kb in range(2):
            for gi in range(2):
                pr = (64 * gi, 64 * gi + 64)
                g = 2 * gp + gi
                mm(pso(p=pr), VT(n + kb, (64 * g, 64 * g + 64)), pts[(gi, kb)](), kb == 0, kb == 1)
            for gi in range(2):
                pr = (64 * gi, 64 * gi + 64)
                mm(psd(p=pr), ONB((0, 64)), pts[(gi, kb)](), kb == 0, kb == 1)

    def unit_b(ui):
        n, gp = units[ui]
        own_c0 = MC0 + 128 * n
        cs = (gp * 4, gp * 4 + 4)
        pso = ps_f32(4 + ui % 2, 1, (4, 128))
        psd = ps_f32(6 + ui % 2, 1, (4, 128))
        rec = REC[ui % 2]
        tt("dve", rec(), psd(), bc(ESK(cs), 2, (128, 4, 128)), ALU.add)
        act(rec(), rec(), AF.Ln)
        act(rec(), rec(), AF.Exp, scale=-1.0)
        tt("dve", AT(cs, (own_c0 - MC0, own_c0 - MC0 + 128)), pso(), rec(), ALU.mult)

    if stop == 'P8a':
        return end(('cc', CC))
    CKB = sb(O_HT + 20480, BF16, (SPC, 256))
    CVB = sb(O_HT + 22528, BF16, (SPC, 256))
    KTS = sb(O_HT + 24576, BF16, (SPC, 2, 128))
    PTS = sb(O_HT + 26624, BF16, (64,))
    RECS = sb(O_HT + 26752, F32, (8, SPC))
    dma("pool", CKB(p=(0, 127)), R(ck[:, 1:128, :].rearrange("s k d -> k s d")), "cks", group=True)
    dma("pool", CVB(p=(0, 127)), R(cv[:, 1:128, :].rearrange("s k d -> k s d")), "cks", group=True)
    for s in range(SPC):
        dma("sp", CVB(s, p=(127, 128)), VB_S(p=(s, s + 1)), "cvs", group=True)
    unit_a(0)
    for ui in range(16):
        if ui + 1 < 16:
            unit_a(ui + 1)
        unit_b(ui)
        if ui % 2 == 1:
            ln_chunk_a(ui // 2)
    ptk = ps_bf16(0, (8, 128))
    for s in range(SPC):
        for c in range(2):
            tr(ptk(2 * s + c, (0, 127)), CKB(s, (128 * c, 128 * c + 128), p=(0, 127)), IDB((0, 127), p=(0, 127)))
    ptk3 = Buf("ps", ptk.ap.rearrange("p (s c) k -> p s c k", c=2), ptk.off, (SPC, 2, 128), 2)
    cp("act", KTS((0, SPC), (0, 2), (0, 127)), ptk3((0, SPC), (0, 2), (0, 127)))
    ktn = KT((0, 2), (SC0, SC0 + SPC))
    cp("dve", Ref(KTS.ap[:, :, :, 127], KTS().rngs), Ref(ktn.ap.rearrange("p c s -> p s c"), ktn.rngs))
    pss2 = [ps_f32(1, 1, (64,)), ps_f32(4, 1, (64,))]
    for s in range(SPC):
        for g in range(4):
            pr = (64 * (g % 2), 64 * (g % 2) + 64)
            cs = ((g // 2) * 4, (g // 2) * 4 + 4)
            mm(pss2[g % 2]((16 * s + 4 * g, 16 * s + 4 * g + 4)), KTS(s, g // 2, p=pr),
               QT(cs, (1024 + s, 1025 + s), p=pr), True, True)
    PTS4 = Buf("sb", PTS.ap.rearrange("p (a g h) -> p a g h", g=2, h=4), PTS.off, (8, 2, 4), 2)
    for hf in range(2):
        src_ = pss2[hf]()
        src4 = Ref(src_.ap.rearrange("p (a g h) -> p a g h", g=2, h=4)[:, :, hf, :], src_.rngs)
        dst_ = PTS()
        act(Ref(PTS4.ap[:, :, hf, :], dst_.rngs), src4, AF.Exp, scale=0.125)
    psos = ps_f32(2, 1, (8, SPC))
    psds = ps_f32(3, 1, (8, SPC))
    for s in range(SPC):
        for g in range(4):
            pr = (64 * (g % 2), 64 * (g % 2) + 64)
            cs = ((g // 2) * 4, (g // 2) * 4 + 4)
            rhs = PTS((16 * s + 4 * g, 16 * s + 4 * g + 4))
            mm(psos(cs, (s, s + 1), p=pr), CVB(s, (64 * g, 64 * g + 64)), rhs, True, True)
            mm(psds(cs, (s, s + 1), p=pr), ONB((0, 64)), rhs, True, True)
    tt("dve", RECS(), psds(), bc(ESK(), 2, (128, 8, SPC)), ALU.add)
    act(RECS(), RECS(), AF.Ln)
    act(RECS(), RECS(), AF.Exp, scale=-1.0)
    tt("dve", AT((0, 8), (1024, 1024 + SPC)), psos(), RECS(), ALU.mult)
    memset("pool", AT((0, 8), (NM - 1, NM)), 0.0)
    for c in range(8):
        ln_chunk_b(c)
    if stop == 'P8':
        return end(('ct', CT))
    if stop == 'P6a':
        return end(('at', AT))

    if stop == 'P6':
        return end(('at', AT))

    WSO = [sb(O_QT + 4096 * i, BF16, (16, 128)) for i in range(3)]
    SQ2 = [sb(O_TMP + 4128 + i * 2080, BF16, (NM,)) for i in range(2)]
    stm = ps_f32(6, 2, (2, 512))

    def norm2_stats(dc):
        sq = SQ2[dc % 2]
        if dc % 4 != 3:
            act(sq((0, 1024)), XT(dc, (0, 1024)), AF.Square)
        else:
            tt("dve", sq((0, 1024)), XT(dc, (0, 1024)), XT(dc, (0, 1024)), ALU.mult)
        for hfc in range(2):
            mm(stm(hfc), ONB(), sq((512 * hfc, 512 * hfc + 512)), dc == 0, dc == 15)

    pending = [wload(WSO, wo_d[0], "wso"), wload(WSO, wo_d[1], "wso")]
    for dc in range(16):
        w = pending.pop(0)
        if dc + 2 < 16:
            pending.append(wload(WSO, wo_d[dc + 2], "wso"))
        accb = acc3(next_slot())
        for kc in range(16):
            src = AT if kc < 8 else CT
            for pi, (a, b) in enumerate(PM):
                mm(accb(pi, (0, 343)), w(kc), src(kc % 8, (a, b)), kc == 0, kc == 15)
        x3 = sub3(XT, dc, 343)
        tt("dve", x3(), accb((0, 3), (0, 343)), x3(), ALU.add)
        if dc >= 1:
            norm2_stats(dc - 1)
    norm2_stats(15)
    if stop == 'P9':
        return end(('xt', XT))

    SQS = sb(O_TMP + 4128, BF16, (16, 5))
    sts = ps_f32(0, 1, (5,))
    act(SQS(), XT((0, 16), (1024, NM)), AF.Square)
    for c in range(16):
        mm(sts(), ONB(), SQS(c), c == 0, c == 15)
    RS2f = sb(O_TMP, F32, (NM,))
    act(RS2f((0, 1024)), Ref(stm().ap.rearrange("p a b -> p (a b)"), stm().rngs), AF.Ln, bias=EPS, scale=1.0 / D)
    act(RS2f((1024, NM)), sts(), AF.Ln, bias=EPS, scale=1.0 / D)
    act(RS2f(), RS2f(), AF.Exp, scale=-0.5)
    for c in range(16):
        stt(HM(c), XT(c), G2(c), RS2f(), ALU.mult, ALU.mult)
    if stop == 'P10':
        return end(('hm', HM))
    HID = [sb(O_RA + i * 8256, BF16, (GFF, NM)) for i in range(2)]
    WUP = [sb(O_QT + 12288 + 4096 * i, BF16, (16, 128)) for i in range(3)]
    WDN = [sb(O_QT + 24576 + 4096 * i, BF16, (16, 128)) for i in range(2 * GFF)]
    assert O_QT + 24576 + 4096 * 2 * GFF <= O_TMP
    YSB = [sb(O_RA + 16512 + 4608 * i, F32, (9, 128)) for i in range(3)]
    NG = NFF // GFF
    wupi = [0]

    def wup_load(f):
        i = wupi[0] % 3
        wupi[0] += 1
        dma("pool", WUP[i](), R(wup_d[f].rearrange("p (k m) -> p k m", m=128)), "wup%d" % i)
        return WUP[i]

    def wdn_load(f):
        i = f % (2 * GFF)
        dma("pool", WDN[i](), R(wdn_d[f].rearrange("p (k m) -> p k m", m=128)), "wdn%d" % i)
        return WDN[i]

    upq = [wup_load(0), wup_load(1)]
    nextup = [2]

    def up_group(g):
        hid = HID[g % 2]
        for fi in range(GFF):
            w = upq.pop(0)
            if nextup[0] < NFF:
                upq.append(wup_load(nextup[0]))
                nextup[0] += 1
            accb = acc3(next_slot())
            for kc in range(16):
                for pi, (a, b) in enumerate(PM):
                    mm(accb(pi, (0, 343)), w(kc), HM(kc, (a, b)), kc == 0, kc == 15)
            act(sub3(hid, fi, 343)(), accb((0, 3), (0, 343)), AF.Relu)
            tt("dve", hid(fi), hid(fi), hid(fi), ALU.mult)

    def out_chunk(dc):
        ysb = YSB[dc % 3]
        for q4 in range(2):
            pt = ps_f32(6 + q4, 1, (4, 128))
            for j in range(4):
                t = 4 * q4 + j
                tr(pt(j), XT(dc, (128 * t, 128 * t + 128)), IDF())
            cp("act", ysb((4 * q4, 4 * q4 + 4)), pt())
        pt = ps_f32(6, 1, (4, 128))
        tr(pt(0, p=(0, SPC)), XT(dc, (1024, 1024 + SPC)), IDF())
        cp("act", ysb(8, p=(0, SPC)), pt(0, p=(0, SPC)))
        dma("sp", R(yp.rearrange("(t p) d -> p t d", p=128)[:, :, 128 * dc:128 * dc + 128]), ysb((0, 8)),
            "ys%d" % (dc % 3), final=True)
        dma("sp", R(ys[:, 128 * dc:128 * dc + 128]), ysb(8, p=(0, SPC)), "ysb%d" % (dc % 3), final=True)

    def down_group(g, wd, last):
        hid = HID[g % 2]
        for dc in range(16):
            accb = acc3(next_slot())
            for fi in range(GFF):
                for pi, (a, b) in enumerate(PM):
                    mm(accb(pi, (0, 343)), wd[fi](dc), hid(fi, (a, b)), fi == 0, fi == GFF - 1)
            x3 = sub3(XT, dc, 343)
            tt("dve", x3(), accb((0, 3), (0, 343)), x3(), ALU.add)
            if last and dc >= 1:
                out_chunk(dc - 1)
        if last:
            out_chunk(15)

    up_group(0)
    wdq = {0: [wdn_load(f) for f in range(GFF)]}
    for g in range(NG):
        if g + 1 < NG:
            up_group(g + 1)
            wdq[g + 1] = [wdn_load((g + 1) * GFF + fi) for fi in range(GFF)]
        down_group(g, wdq.pop(g), g == NG - 1)

    return end()


def _host_weights(w_in, w_out, w_up, w_down):
    w_in = np.asarray(w_in[0], dtype=np.float32)
    w_out = np.asarray(w_out[0], dtype=np.float32)
    w_up = np.asarray(w_up[0], dtype=np.float32)
    w_down = np.asarray(w_down[0], dtype=np.float32)
    qcols = np.concatenate([np.arange(64) + 64 * head_of(c, hf) for c in range(8) for hf in range(2)])
    wq = np.ascontiguousarray(w_in[:, qcols].reshape(16, 128, 1024))
    wkv = np.ascontiguousarray(w_in[:, 1024:1536].reshape(16, 128, 512))
    wu = np.empty((16, 128, 16, 128), np.float32)
    for c in range(8):
        for j, basec in enumerate((1536, 2560)):
            blk = w_in[:, basec + 128 * c: basec + 128 * c + 128]
            wu[2 * c + j] = blk.reshape(16, 128, 128).transpose(1, 0, 2)
    wu = wu.reshape(16, 128, 2048)
    rows = np.concatenate([np.arange(64) + 64 * head_of(c, hf) for c in range(8) for hf in range(2)]
                          + [np.arange(1024, 2048)])
    wo_p = w_out[rows]
    wo = np.ascontiguousarray(wo_p.reshape(16, 128, 16, 128).transpose(2, 1, 0, 3)).reshape(16, 128, 2048)
    wup = np.ascontiguousarray(w_up.reshape(16, 128, NFF, 128).transpose(2, 1, 0, 3)).reshape(NFF, 128, 2048)
    wdn = np.ascontiguousarray(w_down.reshape(NFF, 128, 2048))
    return wq, wkv, wu, wo, wup, wdn


def _host_par(norm_mix_g, q_norm_g, k_norm_g, sinks, conv_w, conv_b, conv_ln_g, conv_ln_b, norm_mlp_g):
    par = np.zeros((128, 640), np.float32)
    par[:, 0:16] = np.asarray(norm_mix_g[0]).reshape(16, 128).T
    par[:, 16:32] = np.asarray(norm_mlp_g[0]).reshape(16, 128).T
    cw = np.asarray(conv_w[0])
    par[:, 32:280] = cw.reshape(31, 8, 128).transpose(2, 1, 0).reshape(128, 248)
    par[:, 280:288] = np.asarray(conv_b[0]).reshape(8, 128).T
    par[:, 288:296] = np.asarray(conv_ln_g[0]).reshape(8, 128).T
    par[:, 296:304] = np.asarray(conv_ln_b[0]).reshape(8, 128).T
    sk = np.asarray(sinks[0])
    for c in range(8):
        par[0:64, 304 + c] = sk[head_of(c, 0)]
        par[64:128, 304 + c] = sk[head_of(c, 1)]
    gq = np.asarray(q_norm_g[0])
    gk = np.asarray(k_norm_g[0])
    par[:, 320:384] = gq[None, :]
    par[:, 384:448] = gk[None, :]
    par[:, 448:512] = np.concatenate([gq[32:], gq[:32]])[None, :]
    par[:, 512:576] = np.concatenate([gk[32:], gk[:32]])[None, :]
    return par


def _host_tab(half):
    inv = (10000.0 ** (-np.arange(32, dtype=np.float32) / np.float32(32))).astype(np.float32)
    tab = np.zeros((128, 2, 10, 64), np.float32)
    for ti in range(10):
        if ti == 0:
            pos = half * 1024 - 128 + np.arange(128)
            pos = np.maximum(pos, 0)
        elif ti == 9:
            pos = np.full(128, PAST)
        else:
            pos = half * 1024 + (ti - 1) * 128 + np.arange(128)
        ang = pos.astype(np.float32)[:, None] * inv[None, :]
        ang = ang.astype(np.float64)
        c = np.cos(ang).astype(np.float32)
        s = np.sin(ang).astype(np.float32)
        tab[:, 0, ti, :32] = c
        tab[:, 0, ti, 32:] = c
        tab[:, 1, ti, :32] = -s
        tab[:, 1, ti, 32:] = s
    return tab.reshape(128, 1280)


def _host_cst(half):
    NEGB = -30000.0
    cst = np.zeros((128, 512), np.float32)
    cst[:, 0:128] = np.eye(128, dtype=np.float32)
    k = np.arange(128)[:, None]
    q = np.arange(128)[None, :]
    cst[:, 128:256] = np.where(k <= q, 0.0, NEGB)
    cst[:, 256:384] = np.where(k > q, 0.0, NEGB)
    cst[:, 384:512] = np.where(k > q, 0.0, NEGB) if half == 1 else NEGB
    return cst


_NC_CACHE = {}


def kernel(x_prompt, x_sample, cache_k, cache_v, state_conv, norm_mix_g, w_in, q_norm_g, k_norm_g, sinks,
           conv_w, conv_b, conv_ln_g, conv_ln_b, w_out, norm_mlp_g, w_up, w_down):
    x_prompt = np.asarray(x_prompt, np.float32)
    x_sample = np.asarray(x_sample, np.float32)
    cache_k = np.asarray(cache_k, np.float32)
    cache_v = np.asarray(cache_v, np.float32)
    state_conv = np.asarray(state_conv, np.float32)
    wq, wkv, wu, wo, wup, wdn = _host_weights(w_in, w_out, w_up, w_down)
    par = _host_par(norm_mix_g, q_norm_g, k_norm_g, sinks, conv_w, conv_b, conv_ln_g, conv_ln_b, norm_mlp_g)
    if "nc" not in _NC_CACHE:
        _NC_CACHE["nc"] = build_program()
    nc = _NC_CACHE["nc"]
    in_maps = []
    for c in range(NCORE):
        b, half = c // 2, c % 2
        xm = np.ascontiguousarray(x_prompt[b, half * 1024:(half + 1) * 1024])
        if half == 1:
            xh = np.ascontiguousarray(x_prompt[b, 896:1024])
        else:
            xh = np.zeros((128, D), np.float32)
        in_maps.append({
            "xm": xm, "xh": xh,
            "xs": np.ascontiguousarray(x_sample[SPC * c:SPC * c + SPC, 0]),
            "ck": np.ascontiguousarray(cache_k[0, SPC * c:SPC * c + SPC].reshape(SPC, 128, 256)),
            "cv": np.ascontiguousarray(cache_v[0, SPC * c:SPC * c + SPC].reshape(SPC, 128, 256)),
            "sc": np.ascontiguousarray(state_conv[0, SPC * c:SPC * c + SPC]),
            "wq": wq, "wkv": wkv, "wu": wu, "wo": wo, "wup": wup, "wdn": wdn,
            "par": par, "tab": _host_tab(half), "cst": _host_cst(half),
        })
    res = run_bass_kernel_spmd(nc, in_maps, core_ids=list(range(NCORE)))
    r = res.results
    y_prompt = np.empty((NB, SEQ, D), np.float32)
    y_sample = np.empty((NS, 1, D), np.float32)
    nkp = np.empty((1, NB, 128, 4, 64), np.float32)
    nvp = np.empty((1, NB, 128, 4, 64), np.float32)
    ncp = np.empty((1, NB, 30, 1024), np.float32)
    nks = np.empty((1, NS, 128, 4, 64), np.float32)
    nvs = np.empty((1, NS, 128, 4, 64), np.float32)
    ncs = np.empty((1, NS, 30, 1024), np.float32)
    for c in range(NCORE):
        b, half = c // 2, c % 2
        y_prompt[b, half * 1024:(half + 1) * 1024] = r[c]["yp"]
        y_sample[SPC * c:SPC * c + SPC, 0] = r[c]["ys"]
        if half == 1:
            nkp[0, b] = r[c]["nkp"].reshape(128, 4, 64)
            nvp[0, b] = r[c]["nvp"].reshape(128, 4, 64)
            ncp[0, b] = r[c]["ncp"]
        nks[0, SPC * c:SPC * c + SPC] = r[c]["nks"].reshape(SPC, 128, 4, 64)
        nvs[0, SPC * c:SPC * c + SPC] = r[c]["nvs"].reshape(SPC, 128, 4, 64)
        ncs[0, SPC * c:SPC * c + SPC] = r[c]["ncs"]
    return (y_prompt, y_sample, nkp, nvp, ncp, nks, nvs, ncs)
```

```python
import bisect
import numpy as np
import ml_dtypes
import concourse.bass as bass
import concourse.mybir as mybir
from concourse.bass_utils import run_bass_kernel_spmd

F32 = mybir.dt.float32
BF16 = mybir.dt.bfloat16
U8 = mybir.dt.uint8
AF = mybir.ActivationFunctionType
ALU = mybir.AluOpType
AX = mybir.AxisListType

D = 2048
NCH = 16
SEQ = 2048
NB = 4
NS = 32
DFF = 8192
NFF = 64
EPS = 1e-6
PAST = 16384
NCORE = 8
SPC = 4

NCOL = 1158
MC0 = 129
NM = 1029
SC0 = 1153
PF = [(0, 386), (386, 772), (772, 1158)]
PM = [(0, 343), (343, 686), (686, 1029)]
TT = [("h", 1, 128)] + [("m%d" % t, MC0 + 128 * t, 128) for t in range(8)] + [("s", SC0, 4)]
GFF = 4


def head_of(c, half):
    if c < 4:
        return c if half == 0 else 4 + c
    return 8 + (c - 4) if half == 0 else 12 + (c - 4)


class Ins:
    __slots__ = ("eng", "fn", "deps", "inc", "count", "key", "is_dma", "group", "name")

    def __init__(self, eng, fn, name=""):
        self.eng = eng
        self.fn = fn
        self.deps = set()
        self.inc = False
        self.count = 0
        self.key = None
        self.is_dma = False
        self.group = False
        self.name = name


class Space:
    def __init__(self):
        self.bp = [0, 1 << 40]
        self.st = {0: [None, []]}

    def _split(self, x):
        i = bisect.bisect_left(self.bp, x)
        if self.bp[i] == x:
            return
        prev = self.bp[i - 1]
        w, r = self.st[prev]
        self.bp.insert(i, x)
        self.st[x] = [w, list(r)]

    def segs(self, a, b):
        self._split(a)
        self._split(b)
        i = bisect.bisect_left(self.bp, a)
        while self.bp[i] < b:
            yield self.st[self.bp[i]]
            i += 1


class Ref:
    __slots__ = ("ap", "rngs")

    def __init__(self, ap, rngs=()):
        self.ap = ap
        self.rngs = list(rngs)


class Buf:
    def __init__(self, space, base_ap, off, shape, esz):
        self.space = space
        self.ap = base_ap
        self.off = off
        self.shape = tuple(shape)
        self.esz = esz

    def __call__(self, *idx, p=None):
        idx = list(idx) + [None] * (len(self.shape) - len(idx))
        sl = []
        norm = []
        for i, n in zip(idx, self.shape):
            if i is None:
                sl.append(slice(None))
                norm.append((0, n))
            elif isinstance(i, tuple):
                assert 0 <= i[0] < i[1] <= n, (i, n)
                sl.append(slice(i[0], i[1]))
                norm.append(i)
            else:
                assert 0 <= i < n, (i, n)
                sl.append(i)
                norm.append((i, i + 1))
        psl = slice(None) if p is None else slice(p[0], p[1])
        ap = self.ap[(psl,) + tuple(sl)]
        strides = []
        s = 1
        for n in reversed(self.shape):
            strides.append(s)
            s *= n
        strides = strides[::-1]
        rngs = []
        dims = len(self.shape)

        def rec(d, base):
            if d == dims - 1:
                lo, hi = norm[d]
                rngs.append([base + lo, base + hi])
                return
            lo, hi = norm[d]
            inner_full = all(norm[k] == (0, self.shape[k]) for k in range(d + 1, dims))
            if inner_full:
                rngs.append([base + lo * strides[d], base + hi * strides[d]])
                return
            for i in range(lo, hi):
                rec(d + 1, base + i * strides[d])

        rec(0, 0)
        rngs.sort()
        merged = []
        for a, b in rngs:
            if merged and merged[-1][1] >= a:
                merged[-1][1] = max(merged[-1][1], b)
            else:
                merged.append([a, b])
        out = [(self.space, self.off + a * self.esz, self.off + b * self.esz) for a, b in merged]
        return Ref(ap, out)


class Prog:
    ENGS = ["pe", "act", "dve", "pool", "sp"]

    def __init__(self):
        self.streams = {e: [] for e in self.ENGS}
        self.spaces = {"sb": Space(), "ps": Space()}
        self.finals = []
        self.keycount = {}
        self.bank_last = {}
        self.n = 0

    def add(self, eng, fn, reads=(), writes=(), key=None, group=False, final=False, name="", after=()):
        ins = Ins(eng, fn, name)
        ins.deps.update(after)
        for ref in reads:
            for (sp, a, b) in ref.rngs:
                for st in self.spaces[sp].segs(a, b):
                    if st[0] is not None:
                        ins.deps.add(st[0])
        for ref in writes:
            for (sp, a, b) in ref.rngs:
                for st in self.spaces[sp].segs(a, b):
                    if st[0] is not None:
                        ins.deps.add(st[0])
                    ins.deps.update(st[1])
        banks = set()
        for ref in list(reads) + list(writes):
            for (sp, a, b) in ref.rngs:
                if sp == "ps":
                    banks.update(range(a // 2048, (b - 1) // 2048 + 1))
        for bk in banks:
            last = self.bank_last.setdefault(bk, {})
            for e2, it in last.items():
                if e2 != eng:
                    ins.deps.add(it)
            last[eng] = ins
        ins.deps.discard(ins)
        for ref in reads:
            for (sp, a, b) in ref.rngs:
                for st in self.spaces[sp].segs(a, b):
                    st[1].append(ins)
        for ref in writes:
            for (sp, a, b) in ref.rngs:
                for st in self.spaces[sp].segs(a, b):
                    st[0] = ins
                    st[1] = []
        if key is not None:
            ins.is_dma = True
            ins.key = key
            ins.group = group
            self.keycount[key] = self.keycount.get(key, 0) + 16
            ins.count = self.keycount[key]
            ins.inc = True
        if final:
            self.finals.append(ins)
        self.streams[eng].append(ins)
        self.n += 1
        return ins

    def finish(self):
        f = Ins("sp", None, "final")
        f.deps = set(self.finals)
        self.streams["sp"].append(f)

    def emit(self, nc):
        for e in self.ENGS:
            for ins in self.streams[e]:
                keep = set()
                for d in ins.deps:
                    if d.is_dma:
                        keep.add(d)
                    elif d.eng == "pe" and ins.eng == "pe" and not ins.is_dma:
                        continue
                    else:
                        keep.add(d)
                        d.inc = True
                ins.deps = keep
        for e in self.ENGS:
            c = 0
            for ins in self.streams[e]:
                if ins.is_dma:
                    continue
                if ins.inc:
                    c += 1
                    ins.count = c
            assert c < 60000, (e, c)
        sems = {}
        for e in ["pe", "act", "dve", "pool"]:
            sems[e] = nc.alloc_semaphore("s_" + e)
        for k in self.keycount:
            sems[("k", k)] = nc.alloc_semaphore("k_" + str(k))
        streams = self.streams
        keycount = self.keycount

        def replay(ename, eng):
            waited = {}
            for ins in streams[ename]:
                need = {}
                for d in ins.deps:
                    if d.is_dma:
                        s = ("k", d.key)
                        c = keycount[d.key] if d.group else d.count
                    else:
                        s = d.eng
                        c = d.count
                    if c > need.get(s, 0):
                        need[s] = c
                for s, c in need.items():
                    if waited.get(s, 0) >= c:
                        continue
                    eng.wait_ge(sems[s], c)
                    waited[s] = c
                if ins.fn is None:
                    continue
                bi = ins.fn(eng)
                if ins.is_dma:
                    bi.then_inc(sems[("k", ins.key)], 16)
                elif ins.inc:
                    bi.then_inc(sems[ins.eng], 1)

        with nc.Block() as block:
            @block.sync
            def _(e):
                replay("sp", e)

            @block.scalar
            def _(e):
                replay("act", e)

            @block.vector
            def _(e):
                replay("dve", e)

            @block.gpsimd
            def _(e):
                replay("pool", e)

            @block.tensor
            def _(e):
                replay("pe", e)


def build_program(stop=None):
    nc = bass.Bass("TRN2", target_bir_lowering=False)
    P = Prog()
    dumps = []

    def end(*bufs):
        for (name, buf) in bufs:
            ref = buf()
            shp = [128] + list(buf.shape)
            dt = F32 if buf.esz == 4 else BF16
            d = nc.dram_tensor("dbg_" + name, shp, dt, kind="ExternalOutput").ap()
            P.add("sp", (lambda d=d, ref=ref: (lambda e: e.dma_start(out=d, in_=ref.ap)))(), reads=[ref],
                  key="dbg_" + name, final=True)
        P.finish()
        P.emit(nc)
        return nc

    def din(name, shape, dt=F32):
        return nc.dram_tensor(name, list(shape), dt, kind="ExternalInput").ap()

    def dout(name, shape, dt=F32):
        return nc.dram_tensor(name, list(shape), dt, kind="ExternalOutput").ap()

    xm = din("xm", [1024, D])
    xh = din("xh", [128, D])
    xs = din("xs", [SPC, D])
    ck = din("ck", [SPC, 128, 256])
    cv = din("cv", [SPC, 128, 256])
    scv = din("sc", [SPC, 30, 1024])
    wq_d = din("wq", [16, 128, 1024])
    wkv_d = din("wkv", [16, 128, 512])
    wu_d = din("wu", [16, 128, 2048])
    wo_d = din("wo", [16, 128, 2048])
    wup_d = din("wup", [NFF, 128, 2048])
    wdn_d = din("wdn", [NFF, 128, 2048])
    par_d = din("par", [128, 640])
    tab_d = din("tab", [128, 1280])
    cst_d = din("cst", [128, 512])

    yp = dout("yp", [1024, D])
    ys = dout("ys", [SPC, D])
    nkp = dout("nkp", [128, 256])
    nvp = dout("nvp", [128, 256])
    ncp = dout("ncp", [30, 1024])
    nks = dout("nks", [SPC, 128, 256])
    nvs = dout("nvs", [SPC, 128, 256])
    ncs = dout("ncs", [SPC, 30, 1024])

    base = (nc.sbuf_base + 31) // 32 * 32
    ARENA = (nc.sbuf_top - base) // 32 * 32
    arena = nc.alloc_sbuf_tensor_at("arena", [128, ARENA], U8, offset=base)
    psum = nc.alloc_psum_tensor("psum", [128, 8, 512], F32)

    def shaped(ap, shape):
        if len(shape) == 2:
            ap = ap.rearrange("p (a b) -> p a b", b=shape[1])
        elif len(shape) == 3:
            ap = ap.rearrange("p (a b c) -> p a b c", b=shape[1], c=shape[2])
        return ap

    def sb(off, dt, shape):
        esz = 4 if dt == F32 else 2
        n = int(np.prod(shape)) * esz
        assert off % 32 == 0 and off + n <= ARENA, (off, n, ARENA)
        ap = shaped(arena[:, off:off + n].bitcast(dt), shape)
        return Buf("sb", ap, off, shape, esz)

    def ps_f32(b0, nb, shape):
        n = int(np.prod(shape))
        assert n <= nb * 512
        ap = psum[:, b0:b0 + nb, :].rearrange("p a b -> p (a b)")[:, 0:n]
        return Buf("ps", shaped(ap, shape), b0 * 2048, shape, 4)

    def ps_bf16(b0, shape):
        n = int(np.prod(shape))
        assert n <= 1024
        ap = psum[:, b0, :].bitcast(BF16)[:, 0:n]
        return Buf("ps", shaped(ap, shape), b0 * 2048, shape, 2)

    def acc3(slot):
        b0 = 3 * slot
        return Buf("ps", psum[:, b0:b0 + 3, :], b0 * 2048, (3, 512), 4)

    def sub3(buf, idx, w):
        n = buf.shape[1]
        assert n == 3 * w
        return Buf(buf.space, buf.ap[:, idx, :].rearrange("p (a b) -> p a b", b=w),
                   buf.off + idx * n * buf.esz, (3, w), buf.esz)

    o = [0]

    def take(n):
        r = o[0]
        o[0] += (n + 31) // 32 * 32
        return r

    O_XT = take(16 * NM * 4)
    O_HT = take(16 * NCOL * 2)
    O_RA = take(33024)
    O_QT = take(8 * NM * 2)
    O_KT = take(2 * NCOL * 2)
    O_VT = take(10 * 256 * 2)
    O_U = take(8 * NCOL * 2)
    O_AT = take(8 * NM * 2)
    O_TMP = take(8320)
    O_MISC = take(6400)
    assert o[0] <= ARENA, (o[0], ARENA)

    XT = sb(O_XT, F32, (16, NM))
    HT = sb(O_HT, BF16, (16, NCOL))
    HM = sb(O_HT, BF16, (16, NM))
    WQ = sb(O_RA, BF16, (16, 1024))
    CC = sb(O_RA, F32, (8, NM))
    QT = sb(O_QT, BF16, (8, NM))
    KT = sb(O_KT, BF16, (2, NCOL))
    VT = sb(O_VT, BF16, (10, 256))
    XIN = [sb(O_QT + i * 8192, F32, (D,)) for i in range(3)]
    U = sb(O_U, BF16, (8, NCOL))
    WKV = sb(O_U, BF16, (16, 512))
    XTH = sb(O_U, F32, (16, 129))
    RSTD = sb(O_U + 8256, F32, (NCOL,))
    CT = sb(O_U, BF16, (8, NM))
    AT = sb(O_AT, BF16, (8, NM))
    GCQ = sb(O_AT, F32, (10, 64))
    GSQ = sb(O_AT + 2560, F32, (10, 64))
    GCK = sb(O_AT + 5120, F32, (10, 64))
    GSK = sb(O_AT + 7680, F32, (10, 64))
    TABIN = sb(O_AT + 10240, F32, (2, 10, 64))
    m = O_MISC
    IDF = sb(m, F32, (128,)); m += 512
    IDB = sb(m, BF16, (128,)); m += 256
    ONB = sb(m, BF16, (128,)); m += 256
    ONM = sb(m, BF16, (128,)); m += 256
    MBOWN = sb(m, BF16, (128,)); m += 256
    MBPRV = sb(m, BF16, (128,)); m += 256
    MBPR0 = sb(m, BF16, (128,)); m += 256
    PAR = sb(m, F32, (640,)); m += 2560
    ESK = sb(m, F32, (8,)); m += 32
    T_SS = sb(m, F32, (16,)); m += 64
    T_RS = sb(m, F32, (16,)); m += 64
    UF = sb(m, F32, (8, 34)); m += 1088
    VB_S = sb(m, BF16, (256,)); m += 512
    assert m <= O_MISC + 6400, m - O_MISC
    KF_L = sb(O_AT + 12288, F32, (256,))
    VF_L = sb(O_AT + 13312, F32, (256,))
    KF_S = sb(O_AT + 14336, F32, (256,))
    VF_S = sb(O_AT + 15360, F32, (256,))

    G1 = lambda c: PAR((c, c + 1))
    G2 = lambda c: PAR((16 + c, 17 + c))
    CW = lambda c: PAR((32 + 31 * c, 32 + 31 * c + 31))
    CB = lambda c: PAR((280 + c, 281 + c))
    LG = lambda c: PAR((288 + c, 289 + c))
    LB = lambda c: PAR((296 + c, 297 + c))
    SKT = PAR((304, 312))
    GQ = PAR((320, 384))
    GK = PAR((384, 448))
    GQS = PAR((448, 512))
    GKS = PAR((512, 576))

    def dma(eng, out, in_, key, group=False, final=False):
        return P.add(eng, lambda e: e.dma_start(out=out.ap, in_=in_.ap), reads=[in_], writes=[out],
                     key=key, group=group, final=final)

    def act(out, in_, func, bias=0.0, scale=1.0):
        rd = [in_]
        b = bias
        s = scale
        if isinstance(bias, Ref):
            rd.append(bias)
            b = bias.ap
        if isinstance(scale, Ref):
            rd.append(scale)
            s = scale.ap
        return P.add("act", lambda e: e.activation(out=out.ap, in_=in_.ap, func=func, bias=b, scale=s),
                     reads=rd, writes=[out])

    def tt(eng, out, in0, in1, op):
        return P.add(eng, lambda e: e.tensor_tensor(out=out.ap, in0=in0.ap, in1=in1.ap, op=op),
                     reads=[in0, in1], writes=[out])

    def stt(out, in0, sc, in1, op0, op1):
        rd = [in0, in1]
        a = sc
        if isinstance(sc, Ref):
            rd.append(sc)
            a = sc.ap
        return P.add("dve", lambda e: e.scalar_tensor_tensor(out=out.ap, in0=in0.ap, scalar=a, in1=in1.ap,
                                                             op0=op0, op1=op1), reads=rd, writes=[out])

    def cp(eng, out, in_):
        if eng == "act":
            return P.add("act", lambda e: e.copy(out=out.ap, in_=in_.ap), reads=[in_], writes=[out])
        return P.add(eng, lambda e: e.tensor_copy(out=out.ap, in_=in_.ap), reads=[in_], writes=[out])

    def recip(out, in_):
        return P.add("dve", lambda e: e.reciprocal(out=out.ap, in_=in_.ap), reads=[in_], writes=[out])

    def red(out, in_):
        return P.add("dve", lambda e: e.tensor_reduce(out=out.ap, in_=in_.ap, axis=AX.X, op=ALU.add),
                     reads=[in_], writes=[out])

    def memset(eng, out, val):
        return P.add(eng, lambda e: e.memset(out.ap, val), writes=[out])

    def mm(out, lhsT, rhs, start, stop):
        return P.add("pe", lambda e: e.matmul(out.ap, lhsT.ap, rhs.ap, start=start, stop=stop),
                     reads=[lhsT, rhs], writes=[out])

    def tr(out, in_, ident):
        return P.add("pe", lambda e: e.transpose(out.ap, in_.ap, ident.ap), reads=[in_, ident], writes=[out])

    def R(ap):
        return Ref(ap)

    def bc(ref, axis, shape):
        return Ref(ref.ap.unsqueeze(axis).broadcast_to(list(shape)), ref.rngs)

    slotrr = [0]

    def next_slot():
        s = slotrr[0] % 2
        slotrr[0] += 1
        return s

    CST = sb(O_TMP, F32, (512,))
    xin_first = dma("sp", XIN[0](p=(0, TT[0][2])), R(xh), "xin0")
    dma("sp", PAR(), R(par_d), "par", group=True)
    dma("sp", CST(), R(cst_d), "par", group=True)
    dma("sp", TABIN(), R(tab_d.rearrange("p (a b c) -> p a b c", a=2, b=10)), "par", group=True)
    cp("dve", IDF(), CST((0, 128)))
    cp("dve", IDB(), CST((0, 128)))
    cp("dve", MBOWN(), CST((128, 256)))
    cp("dve", MBPRV(), CST((256, 384)))
    cp("dve", MBPR0(), CST((384, 512)))
    memset("dve", ONB(), 1.0)
    memset("dve", ONM(), 1.0 / 1024.0)
    for (GC, GS, g, gs) in ((GCQ, GSQ, GQ, GQS), (GCK, GSK, GK, GKS)):
        tt("pool", GC(), TABIN(0), bc(g, 1, (128, 10, 64)), ALU.mult)
        tt("pool", GS(), TABIN(1), bc(gs, 1, (128, 10, 64)), ALU.mult)
    act(ESK(), SKT, AF.Exp)
    memset("pool", XT((0, 16), (NM - 1, NM)), 0.0)
    memset("pool", VT(9), 0.0)

    XN = [sb(O_TMP + 4096 * i, BF16, (D,)) for i in range(2)]
    memset("pool", HT((0, 16), (0, 1)), 0.0)
    memset("pool", HT((0, 16), (NCOL - 1, NCOL)), 0.0)
    bank = [0]
    xin_dmas = []

    def nbank():
        b = bank[0] % 8
        bank[0] += 1
        return b

    for ti, (tn, c0, nt) in enumerate(TT):
        pp = (0, nt)
        xin = XIN[ti % 3]
        xn = XN[ti % 2]
        src_ = xh if tn == "h" else (xs if tn == "s" else xm[(ti - 1) * 128:ti * 128, :])
        if ti == 0:
            xin_dmas.append(xin_first)
        else:
            xin_dmas.append(dma("sp", xin(p=pp), R(src_), "xin%d" % (ti % 3)))
        ms = T_SS((ti, ti + 1), p=pp)
        rs = T_RS((ti, ti + 1), p=pp)
        P.add("act", (lambda xn=xn, xin=xin, ms=ms, pp=pp: (lambda e: e.activation(
            out=xn(p=pp).ap, in_=xin(p=pp).ap, func=AF.Square, accum_out=ms.ap)))(),
            reads=[xin(p=pp)], writes=[xn(p=pp), ms])
        act(rs, ms, AF.Ln, bias=EPS, scale=1.0 / D)
        act(rs, rs, AF.Exp, scale=-0.5)
        P.add("dve", (lambda xn=xn, xin=xin, rs=rs, pp=pp: (lambda e: e.tensor_scalar(
            out=xn(p=pp).ap, in0=xin(p=pp).ap, scalar1=rs.ap, scalar2=None, op0=ALU.mult)))(),
            reads=[xin(p=pp), rs], writes=[xn(p=pp)])
        if ti == 1:
            for k4 in range(4):
                dma("pool", WKV((4 * k4, 4 * k4 + 4)), R(wkv_d[4 * k4:4 * k4 + 4].rearrange("k p n -> p k n")),
                    "wkv", group=True)
        if tn != "h":
            for c4 in range(4):
                pt = ps_f32(nbank(), 1, (4, 128))
                for j in range(4):
                    c = c4 * 4 + j
                    tr(pt(j, (0, nt)), xin((c * 128, c * 128 + 128), p=pp), IDF((0, nt), p=(0, nt)))
                cp("act", XT((c4 * 4, c4 * 4 + 4), (c0 - MC0, c0 - MC0 + nt)), pt((0, 4), (0, nt)))
        for h8 in range(2):
            pt = ps_bf16(nbank(), (8, 128))
            for j in range(8):
                c = 8 * h8 + j
                tr(pt(j, (0, nt)), xn((c * 128, c * 128 + 128), p=pp), IDB((0, nt), p=(0, nt)))
            tt("dve", HT((8 * h8, 8 * h8 + 8), (c0, c0 + nt)), pt((0, 8), (0, nt)),
               bc(PAR((8 * h8, 8 * h8 + 8)), 2, (128, 8, nt)), ALU.mult)
    for kc in range(16):
        P.add("pool", (lambda kc=kc: (lambda e: e.dma_start(out=WQ(kc).ap, in_=wq_d[kc])))(),
              writes=[WQ(kc)], key="wq", group=True, after=xin_dmas[-3:])
    if stop == 'P2':
        return end(('ht', HT))

    def rms_stats(xparts, sqbufs, pieces, width, slot):
        accb = acc3(slot)
        for c in range(16):
            sq = sqbufs[c % 2]
            for (src, d0, n) in xparts(c):
                if c % 4 in (0, 2):
                    act(sq((d0, d0 + n)), src, AF.Square)
                else:
                    tt("dve" if c % 4 == 1 else "pool", sq((d0, d0 + n)), src, src, ALU.mult)
            for pi, (a, b) in enumerate(pieces):
                mm(accb(pi, (0, width)), ONB(), sq((a, b)), c == 0, c == 15)
        return accb

    class QKBufs:
        pass

    qb = QKBufs()
    qb.SQ = sb(O_TMP, BF16, (16, 64))
    qb.A = sb(O_TMP + 2048, F32, (16, 64))
    qb.A2 = sb(O_U + 4096, F32, (16, 64))
    qb.B = sb(O_U, F32, (16, 64))
    qb.Q = sb(O_TMP + 6144, BF16, (16, 64))
    kb_ = QKBufs()
    kb_.SQ = sb(O_TMP, BF16, (4, 64))
    kb_.B = sb(O_TMP + 1024, F32, (4, 64))
    kb_.A = sb(O_TMP + 2048, F32, (4, 64))
    kb_.C = sb(O_TMP + 3072, F32, (4, 64))
    kb_.Q = sb(O_TMP + 6144, BF16, (4, 64))

    def qk_chain(accv, nh, nt, GC, GS, tix, B_, out_bf, out_f32=None):
        pp = (0, nt)
        hs = (0, nh)
        act(B_.SQ(hs, p=pp), accv(hs, p=pp), AF.Square)
        red(T_SS(hs, p=pp), B_.SQ(hs, p=pp))
        act(T_RS(hs, p=pp), T_SS(hs, p=pp), AF.Ln, bias=EPS, scale=1.0 / 64)
        act(T_RS(hs, p=pp), T_RS(hs, p=pp), AF.Exp, scale=-0.5)
        tt("dve", B_.A(hs, p=pp), accv(hs, p=pp), bc(GC(tix, p=pp), 1, (nt, nh, 64)), ALU.mult)
        for d0, d1 in ((0, 32), (32, 64)):
            s0, s1 = (32, 64) if d0 == 0 else (0, 32)
            tt("dve", B_.B(hs, (d0, d1), p=pp), accv(hs, (s0, s1), p=pp),
               bc(GS(tix, (d0, d1), p=pp), 1, (nt, nh, 32)), ALU.mult)
        tt("pool", B_.A(hs, p=pp), B_.A(hs, p=pp), B_.B(hs, p=pp), ALU.add)
        rsb = bc(T_RS(hs, p=pp), 2, (nt, nh, 64))
        if out_f32 is not None:
            tt("pool", out_f32, B_.A(hs, p=pp), rsb, ALU.mult)
            cp("pool", out_bf, out_f32)
        else:
            tt("pool", out_bf, B_.A(hs, p=pp), rsb, ALU.mult)

    Q2 = [sb(O_TMP + 6144, BF16, (16, 64)), sb(O_U + 8192, BF16, (16, 64)), sb(O_AT + 10240, BF16, (16, 64))]
    K2 = [sb(O_TMP + 6144 + 512 * i, BF16, (4, 64)) for i in range(3)]

    def fin_qk(qbuf, nchunk, dst, ti, c0d, nt):
        pt = ps_bf16(6 + ti % 2, (8, 128))
        qf = Buf("sb", qbuf.ap.rearrange("p h d -> p (h d)"), qbuf.off, (qbuf.shape[0] * 64,), 2)
        for c in range(nchunk):
            tr(pt(c, (0, nt)), qf((c * 128, c * 128 + 128), p=(0, nt)), IDB((0, nt), p=(0, nt)))
        cp("act", dst((0, nchunk), (c0d, c0d + nt)), pt((0, nchunk), (0, nt)))

    pend = []
    for ti, (tn, c0, nt) in enumerate(TT):
        pp = (0, nt)
        slot = next_slot()
        acck = ps_f32(3 * slot, 1, (8, 64))
        for kc in range(16):
            mm(acck(p=pp), HT(kc, (c0, c0 + nt)), WKV(kc), kc == 0, kc == 15)
        if tn == "s":
            cp("act", VB_S(p=pp), acck((4, 8), p=pp))
            cp("act", VF_S(p=pp), acck((4, 8), p=pp))
        else:
            cp("act", VT(ti, p=pp), acck((4, 8), p=pp))
            if tn == "m7":
                cp("act", VF_L(), acck((4, 8)))
        kf = KF_L() if tn == "m7" else (KF_S(p=pp) if tn == "s" else kb_.C(p=pp))
        kf4 = Ref(kf.ap.rearrange("p (h d) -> p h d", d=64), kf.rngs) if tn in ("m7", "s") else kf
        kout = K2[ti % 3]
        qk_chain(acck, 4, nt, GCK, GSK, ti, kb_, kout(p=pp), out_f32=kf4)
        if len(pend) == 2:
            fin_qk(*pend.pop(0))
        pend.append((kout, 2, KT, ti, c0, nt))
    kpend = pend
    order = [2 * c + j for c in range(8) for j in (1, 0)]
    WSUF = [sb(O_U + 10240 + 4096 * i, BF16, (16, 128)) for i in range(2)]
    for i in range(2):
        dma("pool", WSUF[i](), R(wu_d[order[i]].rearrange("p (k m) -> p k m", m=128)), "wsuf%d" % i)
    pend = []
    qpend = pend
    for ti, (tn, c0, nt) in enumerate(TT):
        if tn == "h":
            continue
        pp = (0, nt)
        slot = next_slot()
        accq = ps_f32(3 * slot, 2, (16, 64))
        for kc in range(16):
            for hf in range(2):
                mm(accq((8 * hf, 8 * hf + 8), p=pp), HT(kc, (c0, c0 + nt)), WQ(kc, (512 * hf, 512 * hf + 512)),
                   kc == 0, kc == 15)
        while kpend:
            fin_qk(*kpend.pop(0))
        qbt = QKBufs()
        qbt.SQ, qbt.B = qb.SQ, qb.B
        qbt.A = qb.A if ti % 2 == 0 else qb.A2
        qout = Q2[ti % 3]
        qk_chain(accq, 16, nt, GCQ, GSQ, ti, qbt, qout(p=pp))
        if len(pend) == 2:
            fin_qk(*pend.pop(0))
        pend.append((qout, 8, QT, ti, c0 - MC0, nt))

    memset("pool", KT((0, 2), (0, 1)), 0.0)
    memset("pool", KT((0, 2), (NCOL - 1, NCOL)), 0.0)
    memset("pool", QT((0, 8), (NM - 1, NM)), 0.0)

    dma("sp", R(nkp), KF_L(), "kvout", group=True, final=True)
    dma("sp", R(nvp), VF_L(), "kvout", group=True, final=True)
    dma("sp", R(nks[:, 127, :]), KF_S(p=(0, SPC)), "kvout", group=True, final=True)
    dma("sp", R(nvs[:, 127, :]), VF_S(p=(0, SPC)), "kvout", group=True, final=True)
    dma("sp", R(nks[:, 0:127, :]), R(ck[:, 1:128, :]), "d2d", group=True, final=True)
    dma("sp", R(nvs[:, 0:127, :]), R(cv[:, 1:128, :]), "d2d", group=True, final=True)
    dma("sp", R(ncs[:, 0:29, :]), R(scv[:, 1:30, :]), "d2d", group=True, final=True)
    if stop == 'P4':
        return end(('qt', QT), ('kt', KT), ('vt', VT))

    WSU = [sb(O_RA + 12384 + 4096 * i, BF16, (16, 128)) for i in range(3)]
    wsi = [0]

    def wload(slots, src_ap, tag, m=128):
        i = wsi[0] % len(slots)
        wsi[0] += 1
        dma("pool", slots[i](), R(src_ap.rearrange("p (k m) -> p k m", m=m)), "%s%d" % (tag, i))
        return slots[i]

    def conv_tap(c, j, col0=0, ncol=NM):
        g0 = MC0 + j - 30 + col0
        uu = U(c, (g0, g0 + ncol))
        cc = CC(c, (col0, col0 + ncol))
        if j == 0:
            P.add("dve", (lambda c=c, uu=uu, cc=cc: (lambda e: e.tensor_scalar(
                out=cc.ap, in0=uu.ap, scalar1=CW(c).ap[:, 0:1], scalar2=CB(c).ap,
                op0=ALU.mult, op1=ALU.add)))(), reads=[uu, CW(c), CB(c)], writes=[cc])
        else:
            P.add("dve", (lambda c=c, uu=uu, cc=cc, j=j: (lambda e: e.scalar_tensor_tensor(
                out=cc.ap, in0=uu.ap, scalar=CW(c).ap[:, j:j + 1], in1=cc.ap,
                op0=ALU.mult, op1=ALU.add)))(), reads=[uu, CW(c), cc], writes=[cc])

    dve_taps = [(c, j) for c in (0, 1) for j in range(31)]
    DGF = sb(O_AT, BF16, (31, 128))
    PU = [(99, 452), (452, 805), (805, 1158)]
    SG = sb(O_TMP, F32, (3, 353))
    order = [2 * c + j for c in range(8) for j in (1, 0)]
    pending = [WSUF[0], WSUF[1]]
    for c in range(8):
        accs = []
        for j in range(2):
            w = pending.pop(0)
            nxt = 2 * c + j + 2
            if nxt < 16:
                pending.append(wload(WSU, wu_d[order[nxt]], "wsu"))
            accb = acc3(next_slot())
            for kc in range(16):
                for pi, (a, b) in enumerate(PU):
                    mm(accb(pi, (0, 353)), w(kc), HT(kc, (a, b)), kc == 0, kc == 15)
            accs.append(accb)
            while qpend:
                fin_qk(*qpend.pop(0))
            if j == 0:
                act(SG(), accb((0, 3), (0, 353)), AF.Sigmoid)
        ur = U(c, (99, NCOL))
        tt("dve", Ref(ur.ap.rearrange("p (a b) -> p a b", b=353), ur.rngs), accs[1]((0, 3), (0, 353)), SG(), ALU.mult)
        tt("dve", UF(c), accs[1](2, (1123 - 805, 1157 - 805)), SG(2, (1123 - 805, 1157 - 805)), ALU.mult)
        if c == 5:
            tt("pool", DGF(), bc(IDB(), 1, (128, 31, 128)), bc(CW(2), 2, (128, 31, 128)), ALU.mult)
        if c >= 1:
            for _ in range(7 if c < 7 else 99):
                if dve_taps:
                    conv_tap(*dve_taps.pop(0))
    if stop == 'P5':
        return end(('u', U), ('uf', UF))

    UTO = sb(O_TMP, F32, (8, 128))
    ptu = ps_f32(6, 2, (8, 128))
    for c in range(8):
        tr(ptu(c, p=(0, 34)), UF(c), IDF())
    cp("act", UTO(p=(0, 34)), ptu(p=(0, 34)))
    dma("sp", R(ncp), Ref(UTO.ap[0:30].rearrange("p a b -> p (a b)"), UTO().rngs), "uto", group=True, final=True)
    dma("sp", R(ncs[:, 29, :]), Ref(UTO.ap[30:34].rearrange("p a b -> p (a b)"), UTO().rngs), "uto", group=True,
        final=True)

    DG = [sb(O_HT + 7936 * i, BF16, (31, 128)) for i in range(2)]
    FS = sb(O_HT + 16384, F32, (8, SPC, 31))
    SCT = sb(O_HT + 20480, F32, (1024,))
    CSM = sb(O_HT + 24576, F32, (8, SPC))
    PRD = sb(O_HT + 24704, F32, (8, SPC, 31))
    assert not dve_taps
    dma("sp", SCT(p=(0, 120)), R(scv.rearrange("s r c -> (s r) c")), "sct", group=True)
    def sample_conv_prep():
        for c in range(8):
            pt = ps_f32(6 + c % 2, 1, (120,))
            tr(pt(), SCT((128 * c, 128 * c + 128), p=(0, 120)), IDF((0, 120), p=(0, 120)))
            pt3 = Buf("ps", pt.ap.rearrange("p (s r) -> p s r", r=30), pt.off, (SPC, 30), 4)
            cp("act", FS(c, (0, SPC), (0, 30)), pt3())
        cp("dve", Ref(FS.ap[:, :, :, 30], FS().rngs), UF((0, 8), (30, 34)))
        cwa = PAR((32, 280))
        cwb = Ref(cwa.ap.rearrange("p (c j) -> p c j", j=31).unsqueeze(2).broadcast_to([128, 8, SPC, 31]), cwa.rngs)
        tt("dve", PRD(), FS(), cwb, ALU.mult)
        red(CSM(), PRD())

    for j in range(31):
        conv_tap(2, j, 0, 172)
    for c in range(2, 8):
        if c == 2:
            dg = DGF
        else:
            dg = DG[c % 2]
            tt("pool", dg(), bc(IDB(), 1, (128, 31, 128)), bc(CW(c), 2, (128, 31, 128)), ALU.mult)
        accb = acc3(next_slot())
        for pi, (a, b) in enumerate(PM):
            a0 = 172 if (c == 2 and pi == 0) else a
            n_ = b - a0
            for j in range(31):
                g0 = MC0 + a0 + j - 30
                mm(accb(pi, (0, n_)), dg(j), U(c, (g0, g0 + n_)), j == 0, j == 30)
        c3 = sub3(CC, c, 343)
        if c == 2:
            act(CC(c, (172, 343)), accb(0, (0, 171)), AF.Identity, bias=CB(c))
            act(c3((1, 3)), accb((1, 3), (0, 343)), AF.Identity, bias=CB(c))
            sample_conv_prep()
        else:
            act(c3(), accb((0, 3), (0, 343)), AF.Identity, bias=CB(c))
    tt("dve", CC((0, 8), (1024, 1024 + SPC)), CSM(), bc(PAR((280, 288)), 2, (128, 8, SPC)), ALU.add)
    if stop == 'P7':
        return end(('cc', CC))

    LB16 = [sb(O_HT + 28672 + 2080 * i, BF16, (NM,)) for i in range(2)]
    LSQ = [sb(O_HT + 32832 + 2080 * i, BF16, (NM,)) for i in range(2)]
    accm = acc3(next_slot())
    accv = acc3(next_slot())
    for c in range(8):
        cb16 = LB16[c % 2]
        csq = LSQ[c % 2]
        cp("dve", cb16(), CC(c))
        if c % 3 == 2:
            tt("dve", csq(), CC(c), CC(c), ALU.mult)
        else:
            act(csq(), CC(c), AF.Square)
        for pi, (a, b) in enumerate(PM):
            mm(accm(pi, (0, 343)), ONM(), cb16((a, b)), c == 0, c == 7)
        for pi, (a, b) in enumerate(PM):
            mm(accv(pi, (0, 343)), ONM(), csq((a, b)), c == 0, c == 7)
    MEAN = sb(O_HT, F32, (3, 343))
    RSL = sb(O_HT + 4128, F32, (3, 343))
    SGL = [sb(O_HT + 8256 + 2080 * i, BF16, (3, 343)) for i in range(2)]
    cp("dve", MEAN(), accm((0, 3), (0, 343)))
    act(RSL(), accm((0, 3), (0, 343)), AF.Square)
    tt("dve", RSL(), accv((0, 3), (0, 343)), RSL(), ALU.subtract)
    P.add("dve", lambda e: e.tensor_scalar(out=RSL().ap, in0=RSL().ap, scalar1=0.0, scalar2=None, op0=ALU.max),
          reads=[RSL()], writes=[RSL()])
    act(RSL(), RSL(), AF.Ln, bias=EPS)
    act(RSL(), RSL(), AF.Exp, scale=-0.5)

    def ln_chunk_a(c):
        c3 = sub3(CC, c, 343)
        tt("pool", c3(), c3(), MEAN(), ALU.subtract)
        tt("pool", c3(), c3(), RSL(), ALU.mult)
        P.add("pool", (lambda c3=c3, c=c: (lambda e: e.tensor_scalar(
            out=c3().ap, in0=c3().ap, scalar1=LG(c).ap, scalar2=LB(c).ap, op0=ALU.mult, op1=ALU.add)))(),
            reads=[c3(), LG(c), LB(c)], writes=[c3()])

    def ln_chunk_b(c):
        c3 = sub3(CC, c, 343)
        sg = SGL[c % 2]
        act(sg(), c3(), AF.Sigmoid)
        tt("dve", sub3(CT, c, 343)(), c3(), sg(), ALU.mult)

    PT = [sb(O_TMP + 1024 * i, BF16, (4, 128)) for i in range(8)]
    REC = [sb(O_HT + 16384 + 2048 * i, F32, (4, 128)) for i in range(2)]
    units = [(n, gp) for n in range(8) for gp in range(2)]

    def unit_a(ui):
        n, gp = units[ui]
        own_c0 = MC0 + 128 * n
        cs = (gp * 4, gp * 4 + 4)
        pso = ps_f32(4 + ui % 2, 1, (4, 128))
        psd = ps_f32(6 + ui % 2, 1, (4, 128))
        pts = {}
        for kb, kc0 in enumerate((own_c0 - 128, own_c0)):
            for gi in range(2):
                pr = (64 * gi, 64 * gi + 64)
                pss = ps_f32(2 * gi + kb, 1, (4, 128))
                mb = MBOWN if kb == 1 else (MBPR0 if n == 0 else MBPRV)
                mm(pss(), KT(gp, (kc0, kc0 + 128), p=pr), QT(cs, (own_c0 - MC0, own_c0 - MC0 + 128), p=pr),
                   True, False)
                mm(pss(), IDB(), bc(mb(), 1, (128, 4, 128)), False, True)
                pt = PT[(ui % 2) * 4 + 2 * gi + kb]
                act(pt(), pss(), AF.Exp, scale=0.125)
                pts[(gi, kb)] = pt
        for kb in range(2):
            for gi in range(2):
                pr = (64 * gi, 64 * gi + 64)
                g = 2 * gp + gi
                mm(pso(p=pr), VT(n + kb, (64 * g, 64 * g + 64)), pts[(gi, kb)](), kb == 0, kb == 1)
            for gi in range(2):
                pr = (64 * gi, 64 * gi + 64)
                mm(psd(p=pr), ONB((0, 64)), pts[(gi, kb)](), kb == 0, kb == 1)

    def unit_b(ui):
        n, gp = units[ui]
        own_c0 = MC0 + 128 * n
        cs = (gp * 4, gp * 4 + 4)
        pso = ps_f32(4 + ui % 2, 1, (4, 128))
        psd = ps_f32(6 + ui % 2, 1, (4, 128))
        rec = REC[ui % 2]
        tt("dve", rec(), psd(), bc(ESK(cs), 2, (128, 4, 128)), ALU.add)
        act(rec(), rec(), AF.Ln)
        act(rec(), rec(), AF.Exp, scale=-1.0)
        tt("dve", AT(cs, (own_c0 - MC0, own_c0 - MC0 + 128)), pso(), rec(), ALU.mult)

    if stop == 'P8a':
        return end(('cc', CC))
    CKB = sb(O_HT + 20480, BF16, (SPC, 256))
    CVB = sb(O_HT + 22528, BF16, (SPC, 256))
    KTS = sb(O_HT + 24576, BF16, (SPC, 2, 128))
    PTS = sb(O_HT + 26624, BF16, (64,))
    RECS = sb(O_HT + 26752, F32, (8, SPC))
    dma("pool", CKB(p=(0, 127)), R(ck[:, 1:128, :].rearrange("s k d -> k s d")), "cks", group=True)
    dma("pool", CVB(p=(0, 127)), R(cv[:, 1:128, :].rearrange("s k d -> k s d")), "cks", group=True)
    for s in range(SPC):
        dma("sp", CVB(s, p=(127, 128)), VB_S(p=(s, s + 1)), "cvs", group=True)
    unit_a(0)
    for ui in range(16):
        if ui + 1 < 16:
            unit_a(ui + 1)
        unit_b(ui)
        if ui % 2 == 1:
            ln_chunk_a(ui // 2)
    ptk = ps_bf16(0, (8, 128))
    for s in range(SPC):
        for c in range(2):
            tr(ptk(2 * s + c, (0, 127)), CKB(s, (128 * c, 128 * c + 128), p=(0, 127)), IDB((0, 127), p=(0, 127)))
    ptk3 = Buf("ps", ptk.ap.rearrange("p (s c) k -> p s c k", c=2), ptk.off, (SPC, 2, 128), 2)
    cp("act", KTS((0, SPC), (0, 2), (0, 127)), ptk3((0, SPC), (0, 2), (0, 127)))
    ktn = KT((0, 2), (SC0, SC0 + SPC))
    cp("dve", Ref(KTS.ap[:, :, :, 127], KTS().rngs), Ref(ktn.ap.rearrange("p c s -> p s c"), ktn.rngs))
    pss2 = [ps_f32(1, 1, (64,)), ps_f32(4, 1, (64,))]
    for s in range(SPC):
        for g in range(4):
            pr = (64 * (g % 2), 64 * (g % 2) + 64)
            cs = ((g // 2) * 4, (g // 2) * 4 + 4)
            mm(pss2[g % 2]((16 * s + 4 * g, 16 * s + 4 * g + 4)), KTS(s, g // 2, p=pr),
               QT(cs, (1024 + s, 1025 + s), p=pr), True, True)
    PTS4 = Buf("sb", PTS.ap.rearrange("p (a g h) -> p a g h", g=2, h=4), PTS.off, (8, 2, 4), 2)
    for hf in range(2):
        src_ = pss2[hf]()
        src4 = Ref(src_.ap.rearrange("p (a g h) -> p a g h", g=2, h=4)[:, :, hf, :], src_.rngs)
        dst_ = PTS()
        act(Ref(PTS4.ap[:, :, hf, :], dst_.rngs), src4, AF.Exp, scale=0.125)
    psos = ps_f32(2, 1, (8, SPC))
    psds = ps_f32(3, 1, (8, SPC))
    for s in range(SPC):
        for g in range(4):
            pr = (64 * (g % 2), 64 * (g % 2) + 64)
            cs = ((g // 2) * 4, (g // 2) * 4 + 4)
            rhs = PTS((16 * s + 4 * g, 16 * s + 4 * g + 4))
            mm(psos(cs, (s, s + 1), p=pr), CVB(s, (64 * g, 64 * g + 64)), rhs, True, True)
            mm(psds(cs, (s, s + 1), p=pr), ONB((0, 64)), rhs, True, True)
    tt("dve", RECS(), psds(), bc(ESK(), 2, (128, 8, SPC)), ALU.add)
    act(RECS(), RECS(), AF.Ln)
    act(RECS(), RECS(), AF.Exp, scale=-1.0)
    tt("dve", AT((0, 8), (1024, 1024 + SPC)), psos(), RECS(), ALU.mult)
    memset("pool", AT((0, 8), (NM - 1, NM)), 0.0)
    for c in range(8):
        ln_chunk_b(c)
    if stop == 'P8':
        return end(('ct', CT))
    if stop == 'P6a':
        return end(('at', AT))

    if stop == 'P6':
        return end(('at', AT))

    WSO = [sb(O_QT + 4096 * i, BF16, (16, 128)) for i in range(3)]
    SQ2 = [sb(O_TMP + 4128 + i * 2080, BF16, (NM,)) for i in range(2)]
    stm = ps_f32(6, 2, (2, 512))

    def norm2_stats(dc):
        sq = SQ2[dc % 2]
        if dc % 4 != 3:
            act(sq((0, 1024)), XT(dc, (0, 1024)), AF.Square)
        else:
            tt("dve", sq((0, 1024)), XT(dc, (0, 1024)), XT(dc, (0, 1024)), ALU.mult)
        for hfc in range(2):
            mm(stm(hfc), ONB(), sq((512 * hfc, 512 * hfc + 512)), dc == 0, dc == 15)

    pending = [wload(WSO, wo_d[0], "wso"), wload(WSO, wo_d[1], "wso")]
    for dc in range(16):
        w = pending.pop(0)
        if dc + 2 < 16:
            pending.append(wload(WSO, wo_d[dc + 2], "wso"))
        accb = acc3(next_slot())
        for kc in range(16):
            src = AT if kc < 8 else CT
            for pi, (a, b) in enumerate(PM):
                mm(accb(pi, (0, 343)), w(kc), src(kc % 8, (a, b)), kc == 0, kc == 15)
        x3 = sub3(XT, dc, 343)
        tt("dve", x3(), accb((0, 3), (0, 343)), x3(), ALU.add)
        if dc >= 1:
            norm2_stats(dc - 1)
    norm2_stats(15)
    if stop == 'P9':
        return end(('xt', XT))

    SQS = sb(O_TMP + 4128, BF16, (16, 5))
    sts = ps_f32(0, 1, (5,))
    act(SQS(), XT((0, 16), (1024, NM)), AF.Square)
    for c in range(16):
        mm(sts(), ONB(), SQS(c), c == 0, c == 15)
    RS2f = sb(O_TMP, F32, (NM,))
    act(RS2f((0, 1024)), Ref(stm().ap.rearrange("p a b -> p (a b)"), stm().rngs), AF.Ln, bias=EPS, scale=1.0 / D)
    act(RS2f((1024, NM)), sts(), AF.Ln, bias=EPS, scale=1.0 / D)
    act(RS2f(), RS2f(), AF.Exp, scale=-0.5)
    for c in range(16):
        stt(HM(c), XT(c), G2(c), RS2f(), ALU.mult, ALU.mult)
    if stop == 'P10':
        return end(('hm', HM))
    HID = [sb(O_RA + i * 8256, BF16, (GFF, NM)) for i in range(2)]
    WUP = [sb(O_QT + 12288 + 4096 * i, BF16, (16, 128)) for i in range(3)]
    WDN = [sb(O_QT + 24576 + 4096 * i, BF16, (16, 128)) for i in range(2 * GFF)]
    assert O_QT + 24576 + 4096 * 2 * GFF <= O_TMP
    YSB = [sb(O_RA + 16512 + 4608 * i, F32, (9, 128)) for i in range(3)]
    NG = NFF // GFF
    wupi = [0]

    def wup_load(f):
        i = wupi[0] % 3
        wupi[0] += 1
        dma("pool", WUP[i](), R(wup_d[f].rearrange("p (k m) -> p k m", m=128)), "wup%d" % i)
        return WUP[i]

    def wdn_load(f):
        i = f % (2 * GFF)
        dma("pool", WDN[i](), R(wdn_d[f].rearrange("p (k m) -> p k m", m=128)), "wdn%d" % i)
        return WDN[i]

    upq = [wup_load(0), wup_load(1)]
    nextup = [2]

    def up_group(g):
        hid = HID[g % 2]
        for fi in range(GFF):
            w = upq.pop(0)
            if nextup[0] < NFF:
                upq.append(wup_load(nextup[0]))
                nextup[0] += 1
            accb = acc3(next_slot())
            for kc in range(16):
                for pi, (a, b) in enumerate(PM):
                    mm(accb(pi, (0, 343)), w(kc), HM(kc, (a, b)), kc == 0, kc == 15)
            act(sub3(hid, fi, 343)(), accb((0, 3), (0, 343)), AF.Relu)
            tt("dve", hid(fi), hid(fi), hid(fi), ALU.mult)

    def out_chunk(dc):
        ysb = YSB[dc % 3]
        for q4 in range(2):
            pt = ps_f32(6 + q4, 1, (4, 128))
            for j in range(4):
                t = 4 * q4 + j
                tr(pt(j), XT(dc, (128 * t, 128 * t + 128)), IDF())
            cp("act", ysb((4 * q4, 4 * q4 + 4)), pt())
        pt = ps_f32(6, 1, (4, 128))
        tr(pt(0, p=(0, SPC)), XT(dc, (1024, 1024 + SPC)), IDF())
        cp("act", ysb(8, p=(0, SPC)), pt(0, p=(0, SPC)))
        dma("sp", R(yp.rearrange("(t p) d -> p t d", p=128)[:, :, 128 * dc:128 * dc + 128]), ysb((0, 8)),
            "ys%d" % (dc % 3), final=True)
        dma("sp", R(ys[:, 128 * dc:128 * dc + 128]), ysb(8, p=(0, SPC)), "ysb%d" % (dc % 3), final=True)

    def down_group(g, wd, last):
        hid = HID[g % 2]
        for dc in range(16):
            accb = acc3(next_slot())
            for fi in range(GFF):
                for pi, (a, b) in enumerate(PM):
                    mm(accb(pi, (0, 343)), wd[fi](dc), hid(fi, (a, b)), fi == 0, fi == GFF - 1)
            x3 = sub3(XT, dc, 343)
            tt("dve", x3(), accb((0, 3), (0, 343)), x3(), ALU.add)
            if last and dc >= 1:
                out_chunk(dc - 1)
        if last:
            out_chunk(15)

    up_group(0)
    wdq = {0: [wdn_load(f) for f in range(GFF)]}
    for g in range(NG):
        if g + 1 < NG:
            up_group(g + 1)
            wdq[g + 1] = [wdn_load((g + 1) * GFF + fi) for fi in range(GFF)]
        down_group(g, wdq.pop(g), g == NG - 1)

    return end()


def _host_weights(w_in, w_out, w_up, w_down):
    w_in = np.asarray(w_in[0], dtype=np.float32)
    w_out = np.asarray(w_out[0], dtype=np.float32)
    w_up = np.asarray(w_up[0], dtype=np.float32)
    w_down = np.asarray(w_down[0], dtype=np.float32)
    qcols = np.concatenate([np.arange(64) + 64 * head_of(c, hf) for c in range(8) for hf in range(2)])
    wq = np.ascontiguousarray(w_in[:, qcols].reshape(16, 128, 1024))
    wkv = np.ascontiguousarray(w_in[:, 1024:1536].reshape(16, 128, 512))
    wu = np.empty((16, 128, 16, 128), np.float32)
    for c in range(8):
        for j, basec in enumerate((1536, 2560)):
            blk = w_in[:, basec + 128 * c: basec + 128 * c + 128]
            wu[2 * c + j] = blk.reshape(16, 128, 128).transpose(1, 0, 2)
    wu = wu.reshape(16, 128, 2048)
    rows = np.concatenate([np.arange(64) + 64 * head_of(c, hf) for c in range(8) for hf in range(2)]
                          + [np.arange(1024, 2048)])
    wo_p = w_out[rows]
    wo = np.ascontiguousarray(wo_p.reshape(16, 128, 16, 128).transpose(2, 1, 0, 3)).reshape(16, 128, 2048)
    wup = np.ascontiguousarray(w_up.reshape(16, 128, NFF, 128).transpose(2, 1, 0, 3)).reshape(NFF, 128, 2048)
    wdn = np.ascontiguousarray(w_down.reshape(NFF, 128, 2048))
    return wq, wkv, wu, wo, wup, wdn


def _host_par(norm_mix_g, q_norm_g, k_norm_g, sinks, conv_w, conv_b, conv_ln_g, conv_ln_b, norm_mlp_g):
    par = np.zeros((128, 640), np.float32)
    par[:, 0:16] = np.asarray(norm_mix_g[0]).reshape(16, 128).T
    par[:, 16:32] = np.asarray(norm_mlp_g[0]).reshape(16, 128).T
    cw = np.asarray(conv_w[0])
    par[:, 32:280] = cw.reshape(31, 8, 128).transpose(2, 1, 0).reshape(128, 248)
    par[:, 280:288] = np.asarray(conv_b[0]).reshape(8, 128).T
    par[:, 288:296] = np.asarray(conv_ln_g[0]).reshape(8, 128).T
    par[:, 296:304] = np.asarray(conv_ln_b[0]).reshape(8, 128).T
    sk = np.asarray(sinks[0])
    for c in range(8):
        par[0:64, 304 + c] = sk[head_of(c, 0)]
        par[64:128, 304 + c] = sk[head_of(c, 1)]
    gq = np.asarray(q_norm_g[0])
    gk = np.asarray(k_norm_g[0])
    par[:, 320:384] = gq[None, :]
    par[:, 384:448] = gk[None, :]
    par[:, 448:512] = np.concatenate([gq[32:], gq[:32]])[None, :]
    par[:, 512:576] = np.concatenate([gk[32:], gk[:32]])[None, :]
    return par


def _host_tab(half):
    inv = (10000.0 ** (-np.arange(32, dtype=np.float32) / np.float32(32))).astype(np.float32)
    tab = np.zeros((128, 2, 10, 64), np.float32)
    for ti in range(10):
        if ti == 0:
            pos = half * 1024 - 128 + np.arange(128)
            pos = np.maximum(pos, 0)
        elif ti == 9:
            pos = np.full(128, PAST)
        else:
            pos = half * 1024 + (ti - 1) * 128 + np.arange(128)
        ang = pos.astype(np.float32)[:, None] * inv[None, :]
        ang = ang.astype(np.float64)
        c = np.cos(ang).astype(np.float32)
        s = np.sin(ang).astype(np.float32)
        tab[:, 0, ti, :32] = c
        tab[:, 0, ti, 32:] = c
        tab[:, 1, ti, :32] = -s
        tab[:, 1, ti, 32:] = s
    return tab.reshape(128, 1280)


def _host_cst(half):
    NEGB = -30000.0
    cst = np.zeros((128, 512), np.float32)
    cst[:, 0:128] = np.eye(128, dtype=np.float32)
    k = np.arange(128)[:, None]
    q = np.arange(128)[None, :]
    cst[:, 128:256] = np.where(k <= q, 0.0, NEGB)
    cst[:, 256:384] = np.where(k > q, 0.0, NEGB)
    cst[:, 384:512] = np.where(k > q, 0.0, NEGB) if half == 1 else NEGB
    return cst


_NC_CACHE = {}


def kernel(x_prompt, x_sample, cache_k, cache_v, state_conv, norm_mix_g, w_in, q_norm_g, k_norm_g, sinks,
           conv_w, conv_b, conv_ln_g, conv_ln_b, w_out, norm_mlp_g, w_up, w_down):
    x_prompt = np.asarray(x_prompt, np.float32)
    x_sample = np.asarray(x_sample, np.float32)
    cache_k = np.asarray(cache_k, np.float32)
    cache_v = np.asarray(cache_v, np.float32)
    state_conv = np.asarray(state_conv, np.float32)
    wq, wkv, wu, wo, wup, wdn = _host_weights(w_in, w_out, w_up, w_down)
    par = _host_par(norm_mix_g, q_norm_g, k_norm_g, sinks, conv_w, conv_b, conv_ln_g, conv_ln_b, norm_mlp_g)
    if "nc" not in _NC_CACHE:
        _NC_CACHE["nc"] = build_program()
    nc = _NC_CACHE["nc"]
    in_maps = []
    for c in range(NCORE):
        b, half = c // 2, c % 2
        xm = np.ascontiguousarray(x_prompt[b, half * 1024:(half + 1) * 1024])
        if half == 1:
            xh = np.ascontiguousarray(x_prompt[b, 896:1024])
        else:
            xh = np.zeros((128, D), np.float32)
        in_maps.append({
            "xm": xm, "xh": xh,
            "xs": np.ascontiguousarray(x_sample[SPC * c:SPC * c + SPC, 0]),
            "ck": np.ascontiguousarray(cache_k[0, SPC * c:SPC * c + SPC].reshape(SPC, 128, 256)),
            "cv": np.ascontiguousarray(cache_v[0, SPC * c:SPC * c + SPC].reshape(SPC, 128, 256)),
            "sc": np.ascontiguousarray(state_conv[0, SPC * c:SPC * c + SPC]),
            "wq": wq, "wkv": wkv, "wu": wu, "wo": wo, "wup": wup, "wdn": wdn,
            "par": par, "tab": _host_tab(half), "cst": _host_cst(half),
        })
    res = run_bass_kernel_spmd(nc, in_maps, core_ids=list(range(NCORE)))
    r = res.results
    y_prompt = np.empty((NB, SEQ, D), np.float32)
    y_sample = np.empty((NS, 1, D), np.float32)
    nkp = np.empty((1, NB, 128, 4, 64), np.float32)
    nvp = np.empty((1, NB, 128, 4, 64), np.float32)
    ncp = np.empty((1, NB, 30, 1024), np.float32)
    nks = np.empty((1, NS, 128, 4, 64), np.float32)
    nvs = np.empty((1, NS, 128, 4, 64), np.float32)
    ncs = np.empty((1, NS, 30, 1024), np.float32)
    for c in range(NCORE):
        b, half = c // 2, c % 2
        y_prompt[b, half * 1024:(half + 1) * 1024] = r[c]["yp"]
        y_sample[SPC * c:SPC * c + SPC, 0] = r[c]["ys"]
        if half == 1:
            nkp[0, b] = r[c]["nkp"].reshape(128, 4, 64)
            nvp[0, b] = r[c]["nvp"].reshape(128, 4, 64)
            ncp[0, b] = r[c]["ncp"]
        nks[0, SPC * c:SPC * c + SPC] = r[c]["nks"].reshape(SPC, 128, 4, 64)
        nvs[0, SPC * c:SPC * c + SPC] = r[c]["nvs"].reshape(SPC, 128, 4, 64)
        ncs[0, SPC * c:SPC * c + SPC] = r[c]["ncs"]
    return (y_prompt, y_sample, nkp, nvp, ncp, nks, nvs, ncs)
```

```python
import bisect
import numpy as np
import ml_dtypes
import concourse.bass as bass
import concourse.mybir as mybir
from concourse.bass_utils import run_bass_kernel_spmd

F32 = mybir.dt.float32
BF16 = mybir.dt.bfloat16
U8 = mybir.dt.uint8
AF = mybir.ActivationFunctionType
ALU = mybir.AluOpType
AX = mybir.AxisListType

D = 2048
NCH = 16
SEQ = 2048
NB = 4
NS = 32
DFF = 8192
NFF = 64
EPS = 1e-6
PAST = 16384
NCORE = 8
SPC = 4

NCOL = 1158
MC0 = 129
NM = 1029
SC0 = 1153
PF = [(0, 386), (386, 772), (772, 1158)]
PM = [(0, 343), (343, 686), (686, 1029)]
TT = [("h", 1, 128)] + [("m%d" % t, MC0 + 128 * t, 128) for t in range(8)] + [("s", SC0, 4)]
GFF = 4


def head_of(c, half):
    if c < 4:
        return c if half == 0 else 4 + c
    return 8 + (c - 4) if half == 0 else 12 + (c - 4)


class Ins:
    __slots__ = ("eng", "fn", "deps", "inc", "count", "key", "is_dma", "group", "name")

    def __init__(self, eng, fn, name=""):
        self.eng = eng
        self.fn = fn
        self.deps = set()
        self.inc = False
        self.count = 0
        self.key = None
        self.is_dma = False
        self.group = False
        self.name = name


class Space:
    def __init__(self):
        self.bp = [0, 1 << 40]
        self.st = {0: [None, []]}

    def _split(self, x):
        i = bisect.bisect_left(self.bp, x)
        if self.bp[i] == x:
            return
        prev = self.bp[i - 1]
        w, r = self.st[prev]
        self.bp.insert(i, x)
        self.st[x] = [w, list(r)]

    def segs(self, a, b):
        self._split(a)
        self._split(b)
        i = bisect.bisect_left(self.bp, a)
        while self.bp[i] < b:
            yield self.st[self.bp[i]]
            i += 1


class Ref:
    __slots__ = ("ap", "rngs")

    def __init__(self, ap, rngs=()):
        self.ap = ap
        self.rngs = list(rngs)


class Buf:
    def __init__(self, space, base_ap, off, shape, esz):
        self.space = space
        self.ap = base_ap
        self.off = off
        self.shape = tuple(shape)
        self.esz = esz

    def __call__(self, *idx, p=None):
        idx = list(idx) + [None] * (len(self.shape) - len(idx))
        sl = []
        norm = []
        for i, n in zip(idx, self.shape):
            if i is None:
                sl.append(slice(None))
                norm.append((0, n))
            elif isinstance(i, tuple):
                assert 0 <= i[0] < i[1] <= n, (i, n)
                sl.append(slice(i[0], i[1]))
                norm.append(i)
            else:
                assert 0 <= i < n, (i, n)
                sl.append(i)
                norm.append((i, i + 1))
        psl = slice(None) if p is None else slice(p[0], p[1])
        ap = self.ap[(psl,) + tuple(sl)]
        strides = []
        s = 1
        for n in reversed(self.shape):
            strides.append(s)
            s *= n
        strides = strides[::-1]
        rngs = []
        dims = len(self.shape)

        def rec(d, base):
            if d == dims - 1:
                lo, hi = norm[d]
                rngs.append([base + lo, base + hi])
                return
            lo, hi = norm[d]
            inner_full = all(norm[k] == (0, self.shape[k]) for k in range(d + 1, dims))
            if inner_full:
                rngs.append([base + lo * strides[d], base + hi * strides[d]])
                return
            for i in range(lo, hi):
                rec(d + 1, base + i * strides[d])

        rec(0, 0)
        rngs.sort()
        merged = []
        for a, b in rngs:
            if merged and merged[-1][1] >= a:
                merged[-1][1] = max(merged[-1][1], b)
            else:
                merged.append([a, b])
        out = [(self.space, self.off + a * self.esz, self.off + b * self.esz) for a, b in merged]
        return Ref(ap, out)


class Prog:
    ENGS = ["pe", "act", "dve", "pool", "sp"]

    def __init__(self):
        self.streams = {e: [] for e in self.ENGS}
        self.spaces = {"sb": Space(), "ps": Space()}
        self.finals = []
        self.keycount = {}
        self.bank_last = {}
        self.n = 0

    def add(self, eng, fn, reads=(), writes=(), key=None, group=False, final=False, name="", after=()):
        ins = Ins(eng, fn, name)
        ins.deps.update(after)
        for ref in reads:
            for (sp, a, b) in ref.rngs:
                for st in self.spaces[sp].segs(a, b):
                    if st[0] is not None:
                        ins.deps.add(st[0])
        for ref in writes:
            for (sp, a, b) in ref.rngs:
                for st in self.spaces[sp].segs(a, b):
                    if st[0] is not None:
                        ins.deps.add(st[0])
                    ins.deps.update(st[1])
        banks = set()
        for ref in list(reads) + list(writes):
            for (sp, a, b) in ref.rngs:
                if sp == "ps":
                    banks.update(range(a // 2048, (b - 1) // 2048 + 1))
        for bk in banks:
            last = self.bank_last.setdefault(bk, {})
            for e2, it in last.items():
                if e2 != eng:
                    ins.deps.add(it)
            last[eng] = ins
        ins.deps.discard(ins)
        for ref in reads:
            for (sp, a, b) in ref.rngs:
                for st in self.spaces[sp].segs(a, b):
                    st[1].append(ins)
        for ref in writes:
            for (sp, a, b) in ref.rngs:
                for st in self.spaces[sp].segs(a, b):
                    st[0] = ins
                    st[1] = []
        if key is not None:
            ins.is_dma = True
            ins.key = key
            ins.group = group
            self.keycount[key] = self.keycount.get(key, 0) + 16
            ins.count = self.keycount[key]
            ins.inc = True
        if final:
            self.finals.append(ins)
        self.streams[eng].append(ins)
        self.n += 1
        return ins

    def finish(self):
        f = Ins("sp", None, "final")
        f.deps = set(self.finals)
        self.streams["sp"].append(f)

    def emit(self, nc):
        for e in self.ENGS:
            for ins in self.streams[e]:
                keep = set()
                for d in ins.deps:
                    if d.is_dma:
                        keep.add(d)
                    elif d.eng == "pe" and ins.eng == "pe" and not ins.is_dma:
                        continue
                    else:
                        keep.add(d)
                        d.inc = True
                ins.deps = keep
        for e in self.ENGS:
            c = 0
            for ins in self.streams[e]:
                if ins.is_dma:
                    continue
                if ins.inc:
                    c += 1
                    ins.count = c
            assert c < 60000, (e, c)
        sems = {}
        for e in ["pe", "act", "dve", "pool"]:
            sems[e] = nc.alloc_semaphore("s_" + e)
        for k in self.keycount:
            sems[("k", k)] = nc.alloc_semaphore("k_" + str(k))
        streams = self.streams
        keycount = self.keycount

        def replay(ename, eng):
            waited = {}
            for ins in streams[ename]:
                need = {}
                for d in ins.deps:
                    if d.is_dma:
                        s = ("k", d.key)
                        c = keycount[d.key] if d.group else d.count
                    else:
                        s = d.eng
                        c = d.count
                    if c > need.get(s, 0):
                        need[s] = c
                for s, c in need.items():
                    if waited.get(s, 0) >= c:
                        continue
                    eng.wait_ge(sems[s], c)
                    waited[s] = c
                if ins.fn is None:
                    continue
                bi = ins.fn(eng)
                if ins.is_dma:
                    bi.then_inc(sems[("k", ins.key)], 16)
                elif ins.inc:
                    bi.then_inc(sems[ins.eng], 1)

        with nc.Block() as block:
            @block.sync
            def _(e):
                replay("sp", e)

            @block.scalar
            def _(e):
                replay("act", e)

            @block.vector
            def _(e):
                replay("dve", e)

            @block.gpsimd
            def _(e):
                replay("pool", e)

            @block.tensor
            def _(e):
                replay("pe", e)


def build_program(stop=None):
    nc = bass.Bass("TRN2", target_bir_lowering=False)
    P = Prog()
    dumps = []

    def end(*bufs):
        for (name, buf) in bufs:
            ref = buf()
            shp = [128] + list(buf.shape)
            dt = F32 if buf.esz == 4 else BF16
            d = nc.dram_tensor("dbg_" + name, shp, dt, kind="ExternalOutput").ap()
            P.add("sp", (lambda d=d, ref=ref: (lambda e: e.dma_start(out=d, in_=ref.ap)))(), reads=[ref],
                  key="dbg_" + name, final=True)
        P.finish()
        P.emit(nc)
        return nc

    def din(name, shape, dt=F32):
        return nc.dram_tensor(name, list(shape), dt, kind="ExternalInput").ap()

    def dout(name, shape, dt=F32):
        return nc.dram_tensor(name, list(shape), dt, kind="ExternalOutput").ap()

    xm = din("xm", [1024, D])
    xh = din("xh", [128, D])
    xs = din("xs", [SPC, D])
    ck = din("ck", [SPC, 128, 256])
    cv = din("cv", [SPC, 128, 256])
    scv = din("sc", [SPC, 30, 1024])
    wq_d = din("wq", [16, 128, 1024])
    wkv_d = din("wkv", [16, 128, 512])
    wu_d = din("wu", [16, 128, 2048])
    wo_d = din("wo", [16, 128, 2048])
    wup_d = din("wup", [NFF, 128, 2048])
    wdn_d = din("wdn", [NFF, 128, 2048])
    par_d = din("par", [128, 640])
    tab_d = din("tab", [128, 1280])
    cst_d = din("cst", [128, 512])

    yp = dout("yp", [1024, D])
    ys = dout("ys", [SPC, D])
    nkp = dout("nkp", [128, 256])
    nvp = dout("nvp", [128, 256])
    ncp = dout("ncp", [30, 1024])
    nks = dout("nks", [SPC, 128, 256])
    nvs = dout("nvs", [SPC, 128, 256])
    ncs = dout("ncs", [SPC, 30, 1024])

    base = (nc.sbuf_base + 31) // 32 * 32
    ARENA = (nc.sbuf_top - base) // 32 * 32
    arena = nc.alloc_sbuf_tensor_at("arena", [128, ARENA], U8, offset=base)
    psum = nc.alloc_psum_tensor("psum", [128, 8, 512], F32)

    def shaped(ap, shape):
        if len(shape) == 2:
            ap = ap.rearrange("p (a b) -> p a b", b=shape[1])
        elif len(shape) == 3:
            ap = ap.rearrange("p (a b c) -> p a b c", b=shape[1], c=shape[2])
        return ap

    def sb(off, dt, shape):
        esz = 4 if dt == F32 else 2
        n = int(np.prod(shape)) * esz
        assert off % 32 == 0 and off + n <= ARENA, (off, n, ARENA)
        ap = shaped(arena[:, off:off + n].bitcast(dt), shape)
        return Buf("sb", ap, off, shape, esz)

    def ps_f32(b0, nb, shape):
        n = int(np.prod(shape))
        assert n <= nb * 512
        ap = psum[:, b0:b0 + nb, :].rearrange("p a b -> p (a b)")[:, 0:n]
        return Buf("ps", shaped(ap, shape), b0 * 2048, shape, 4)

    def ps_bf16(b0, shape):
        n = int(np.prod(shape))
        assert n <= 1024
        ap = psum[:, b0, :].bitcast(BF16)[:, 0:n]
        return Buf("ps", shaped(ap, shape), b0 * 2048, shape, 2)

    def acc3(slot):
        b0 = 3 * slot
        return Buf("ps", psum[:, b0:b0 + 3, :], b0 * 2048, (3, 512), 4)

    def sub3(buf, idx, w):
        n = buf.shape[1]
        assert n == 3 * w
        return Buf(buf.space, buf.ap[:, idx, :].rearrange("p (a b) -> p a b", b=w),
                   buf.off + idx * n * buf.esz, (3, w), buf.esz)

    o = [0]

    def take(n):
        r = o[0]
        o[0] += (n + 31) // 32 * 32
        return r

    O_XT = take(16 * NM * 4)
    O_HT = take(16 * NCOL * 2)
    O_RA = take(33024)
    O_QT = take(8 * NM * 2)
    O_KT = take(2 * NCOL * 2)
    O_VT = take(10 * 256 * 2)
    O_U = take(8 * NCOL * 2)
    O_AT = take(8 * NM * 2)
    O_TMP = take(8320)
    O_MISC = take(6400)
    assert o[0] <= ARENA, (o[0], ARENA)

    XT = sb(O_XT, F32, (16, NM))
    HT = sb(O_HT, BF16, (16, NCOL))
    HM = sb(O_HT, BF16, (16, NM))
    WQ = sb(O_RA, BF16, (16, 1024))
    CC = sb(O_RA, F32, (8, NM))
    QT = sb(O_QT, BF16, (8, NM))
    KT = sb(O_KT, BF16, (2, NCOL))
    VT = sb(O_VT, BF16, (10, 256))
    XIN = [sb(O_QT + i * 8192, F32, (D,)) for i in range(3)]
    U = sb(O_U, BF16, (8, NCOL))
    WKV = sb(O_U, BF16, (16, 512))
    XTH = sb(O_U, F32, (16, 129))
    RSTD = sb(O_U + 8256, F32, (NCOL,))
    CT = sb(O_U, BF16, (8, NM))
    AT = sb(O_AT, BF16, (8, NM))
    GCQ = sb(O_AT, F32, (10, 64))
    GSQ = sb(O_AT + 2560, F32, (10, 64))
    GCK = sb(O_AT + 5120, F32, (10, 64))
    GSK = sb(O_AT + 7680, F32, (10, 64))
    TABIN = sb(O_AT + 10240, F32, (2, 10, 64))
    m = O_MISC
    IDF = sb(m, F32, (128,)); m += 512
    IDB = sb(m, BF16, (128,)); m += 256
    ONB = sb(m, BF16, (128,)); m += 256
    ONM = sb(m, BF16, (128,)); m += 256
    MBOWN = sb(m, BF16, (128,)); m += 256
    MBPRV = sb(m, BF16, (128,)); m += 256
    MBPR0 = sb(m, BF16, (128,)); m += 256
    PAR = sb(m, F32, (640,)); m += 2560
    ESK = sb(m, F32, (8,)); m += 32
    T_SS = sb(m, F32, (16,)); m += 64
    T_RS = sb(m, F32, (16,)); m += 64
    UF = sb(m, F32, (8, 34)); m += 1088
    VB_S = sb(m, BF16, (256,)); m += 512
    assert m <= O_MISC + 6400, m - O_MISC
    KF_L = sb(O_AT + 12288, F32, (256,))
    VF_L = sb(O_AT + 13312, F32, (256,))
    KF_S = sb(O_AT + 14336, F32, (256,))
    VF_S = sb(O_AT + 15360, F32, (256,))

    G1 = lambda c: PAR((c, c + 1))
    G2 = lambda c: PAR((16 + c, 17 + c))
    CW = lambda c: PAR((32 + 31 * c, 32 + 31 * c + 31))
    CB = lambda c: PAR((280 + c, 281 + c))
    LG = lambda c: PAR((288 + c, 289 + c))
    LB = lambda c: PAR((296 + c, 297 + c))
    SKT = PAR((304, 312))
    GQ = PAR((320, 384))
    GK = PAR((384, 448))
    GQS = PAR((448, 512))
    GKS = PAR((512, 576))

    def dma(eng, out, in_, key, group=False, final=False):
        return P.add(eng, lambda e: e.dma_start(out=out.ap, in_=in_.ap), reads=[in_], writes=[out],
                     key=key, group=group, final=final)

    def act(out, in_, func, bias=0.0, scale=1.0):
        rd = [in_]
        b = bias
        s = scale
        if isinstance(bias, Ref):
            rd.append(bias)
            b = bias.ap
        if isinstance(scale, Ref):
            rd.append(scale)
            s = scale.ap
        return P.add("act", lambda e: e.activation(out=out.ap, in_=in_.ap, func=func, bias=b, scale=s),
                     reads=rd, writes=[out])

    def tt(eng, out, in0, in1, op):
        return P.add(eng, lambda e: e.tensor_tensor(out=out.ap, in0=in0.ap, in1=in1.ap, op=op),
                     reads=[in0, in1], writes=[out])

    def stt(out, in0, sc, in1, op0, op1):
        rd = [in0, in1]
        a = sc
        if isinstance(sc, Ref):
            rd.append(sc)
            a = sc.ap
        return P.add("dve", lambda e: e.scalar_tensor_tensor(out=out.ap, in0=in0.ap, scalar=a, in1=in1.ap,
                                                             op0=op0, op1=op1), reads=rd, writes=[out])

    def cp(eng, out, in_):
        if eng == "act":
            return P.add("act", lambda e: e.copy(out=out.ap, in_=in_.ap), reads=[in_], writes=[out])
        return P.add(eng, lambda e: e.tensor_copy(out=out.ap, in_=in_.ap), reads=[in_], writes=[out])

    def recip(out, in_):
        return P.add("dve", lambda e: e.reciprocal(out=out.ap, in_=in_.ap), reads=[in_], writes=[out])

    def red(out, in_):
        return P.add("dve", lambda e: e.tensor_reduce(out=out.ap, in_=in_.ap, axis=AX.X, op=ALU.add),
                     reads=[in_], writes=[out])

    def memset(eng, out, val):
        return P.add(eng, lambda e: e.memset(out.ap, val), writes=[out])

    def mm(out, lhsT, rhs, start, stop):
        return P.add("pe", lambda e: e.matmul(out.ap, lhsT.ap, rhs.ap, start=start, stop=stop),
                     reads=[lhsT, rhs], writes=[out])

    def tr(out, in_, ident):
        return P.add("pe", lambda e: e.transpose(out.ap, in_.ap, ident.ap), reads=[in_, ident], writes=[out])

    def R(ap):
        return Ref(ap)

    def bc(ref, axis, shape):
        return Ref(ref.ap.unsqueeze(axis).broadcast_to(list(shape)), ref.rngs)

    slotrr = [0]

    def next_slot():
        s = slotrr[0] % 2
        slotrr[0] += 1
        return s

    CST = sb(O_TMP, F32, (512,))
    dma("sp", PAR(), R(par_d), "par", group=True)
    dma("sp", CST(), R(cst_d), "par", group=True)
    dma("sp", TABIN(), R(tab_d.rearrange("p (a b c) -> p a b c", a=2, b=10)), "par", group=True)
    cp("dve", IDF(), CST((0, 128)))
    cp("dve", IDB(), CST((0, 128)))
    cp("dve", MBOWN(), CST((128, 256)))
    cp("dve", MBPRV(), CST((256, 384)))
    cp("dve", MBPR0(), CST((384, 512)))
    memset("dve", ONB(), 1.0)
    memset("dve", ONM(), 1.0 / 1024.0)
    for (GC, GS, g, gs) in ((GCQ, GSQ, GQ, GQS), (GCK, GSK, GK, GKS)):
        tt("pool", GC(), TABIN(0), bc(g, 1, (128, 10, 64)), ALU.mult)
        tt("pool", GS(), TABIN(1), bc(gs, 1, (128, 10, 64)), ALU.mult)
    act(ESK(), SKT, AF.Exp)
    memset("pool", XT((0, 16), (NM - 1, NM)), 0.0)
    memset("pool", VT(9), 0.0)

    XN = [sb(O_TMP + 4096 * i, BF16, (D,)) for i in range(2)]
    memset("pool", HT((0, 16), (0, 1)), 0.0)
    memset("pool", HT((0, 16), (NCOL - 1, NCOL)), 0.0)
    bank = [0]
    xin_dmas = []

    def nbank():
        b = bank[0] % 8
        bank[0] += 1
        return b

    for ti, (tn, c0, nt) in enumerate(TT):
        pp = (0, nt)
        xin = XIN[ti % 3]
        xn = XN[ti % 2]
        src_ = xh if tn == "h" else (xs if tn == "s" else xm[(ti - 1) * 128:ti * 128, :])
        xin_dmas.append(dma("sp", xin(p=pp), R(src_), "xin%d" % (ti % 3)))
        ms = T_SS((ti, ti + 1), p=pp)
        rs = T_RS((ti, ti + 1), p=pp)
        P.add("act", (lambda xn=xn, xin=xin, ms=ms, pp=pp: (lambda e: e.activation(
            out=xn(p=pp).ap, in_=xin(p=pp).ap, func=AF.Square, accum_out=ms.ap)))(),
            reads=[xin(p=pp)], writes=[xn(p=pp), ms])
        act(rs, ms, AF.Ln, bias=EPS, scale=1.0 / D)
        act(rs, rs, AF.Exp, scale=-0.5)
        P.add("dve", (lambda xn=xn, xin=xin, rs=rs, pp=pp: (lambda e: e.tensor_scalar(
            out=xn(p=pp).ap, in0=xin(p=pp).ap, scalar1=rs.ap, scalar2=None, op0=ALU.mult)))(),
            reads=[xin(p=pp), rs], writes=[xn(p=pp)])
        if ti == 1:
            for k4 in range(4):
                dma("pool", WKV((4 * k4, 4 * k4 + 4)), R(wkv_d[4 * k4:4 * k4 + 4].rearrange("k p n -> p k n")),
                    "wkv", group=True)
        if tn != "h":
            for c4 in range(4):
                pt = ps_f32(nbank(), 1, (4, 128))
                for j in range(4):
                    c = c4 * 4 + j
                    tr(pt(j, (0, nt)), xin((c * 128, c * 128 + 128), p=pp), IDF((0, nt), p=(0, nt)))
                cp("act", XT((c4 * 4, c4 * 4 + 4), (c0 - MC0, c0 - MC0 + nt)), pt((0, 4), (0, nt)))
        for h8 in range(2):
            pt = ps_bf16(nbank(), (8, 128))
            for j in range(8):
                c = 8 * h8 + j
                tr(pt(j, (0, nt)), xn((c * 128, c * 128 + 128), p=pp), IDB((0, nt), p=(0, nt)))
            tt("dve", HT((8 * h8, 8 * h8 + 8), (c0, c0 + nt)), pt((0, 8), (0, nt)),
               bc(PAR((8 * h8, 8 * h8 + 8)), 2, (128, 8, nt)), ALU.mult)
    for kc in range(16):
        P.add("pool", (lambda kc=kc: (lambda e: e.dma_start(out=WQ(kc).ap, in_=wq_d[kc])))(),
              writes=[WQ(kc)], key="wq", group=True, after=xin_dmas[-3:])
    if stop == 'P2':
        return end(('ht', HT))

    def rms_stats(xparts, sqbufs, pieces, width, slot):
        accb = acc3(slot)
        for c in range(16):
            sq = sqbufs[c % 2]
            for (src, d0, n) in xparts(c):
                if c % 4 in (0, 2):
                    act(sq((d0, d0 + n)), src, AF.Square)
                else:
                    tt("dve" if c % 4 == 1 else "pool", sq((d0, d0 + n)), src, src, ALU.mult)
            for pi, (a, b) in enumerate(pieces):
                mm(accb(pi, (0, width)), ONB(), sq((a, b)), c == 0, c == 15)
        return accb

    class QKBufs:
        pass

    qb = QKBufs()
    qb.SQ = sb(O_TMP, BF16, (16, 64))
    qb.A = sb(O_TMP + 2048, F32, (16, 64))
    qb.A2 = sb(O_U + 4096, F32, (16, 64))
    qb.B = sb(O_U, F32, (16, 64))
    qb.Q = sb(O_TMP + 6144, BF16, (16, 64))
    kb_ = QKBufs()
    kb_.SQ = sb(O_TMP, BF16, (4, 64))
    kb_.B = sb(O_TMP + 1024, F32, (4, 64))
    kb_.A = sb(O_TMP + 2048, F32, (4, 64))
    kb_.C = sb(O_TMP + 3072, F32, (4, 64))
    kb_.Q = sb(O_TMP + 6144, BF16, (4, 64))

    def qk_chain(accv, nh, nt, GC, GS, tix, B_, out_bf, out_f32=None):
        pp = (0, nt)
        hs = (0, nh)
        act(B_.SQ(hs, p=pp), accv(hs, p=pp), AF.Square)
        red(T_SS(hs, p=pp), B_.SQ(hs, p=pp))
        act(T_RS(hs, p=pp), T_SS(hs, p=pp), AF.Ln, bias=EPS, scale=1.0 / 64)
        act(T_RS(hs, p=pp), T_RS(hs, p=pp), AF.Exp, scale=-0.5)
        tt("dve", B_.A(hs, p=pp), accv(hs, p=pp), bc(GC(tix, p=pp), 1, (nt, nh, 64)), ALU.mult)
        for d0, d1 in ((0, 32), (32, 64)):
            s0, s1 = (32, 64) if d0 == 0 else (0, 32)
            tt("dve", B_.B(hs, (d0, d1), p=pp), accv(hs, (s0, s1), p=pp),
               bc(GS(tix, (d0, d1), p=pp), 1, (nt, nh, 32)), ALU.mult)
        tt("pool", B_.A(hs, p=pp), B_.A(hs, p=pp), B_.B(hs, p=pp), ALU.add)
        rsb = bc(T_RS(hs, p=pp), 2, (nt, nh, 64))
        if out_f32 is not None:
            tt("pool", out_f32, B_.A(hs, p=pp), rsb, ALU.mult)
            cp("pool", out_bf, out_f32)
        else:
            tt("pool", out_bf, B_.A(hs, p=pp), rsb, ALU.mult)

    Q2 = [sb(O_TMP + 6144, BF16, (16, 64)), sb(O_U + 8192, BF16, (16, 64)), sb(O_AT + 10240, BF16, (16, 64))]
    K2 = [sb(O_TMP + 6144 + 512 * i, BF16, (4, 64)) for i in range(3)]

    def fin_qk(qbuf, nchunk, dst, ti, c0d, nt):
        pt = ps_bf16(6 + ti % 2, (8, 128))
        qf = Buf("sb", qbuf.ap.rearrange("p h d -> p (h d)"), qbuf.off, (qbuf.shape[0] * 64,), 2)
        for c in range(nchunk):
            tr(pt(c, (0, nt)), qf((c * 128, c * 128 + 128), p=(0, nt)), IDB((0, nt), p=(0, nt)))
        cp("act", dst((0, nchunk), (c0d, c0d + nt)), pt((0, nchunk), (0, nt)))

    pend = []
    for ti, (tn, c0, nt) in enumerate(TT):
        pp = (0, nt)
        slot = next_slot()
        acck = ps_f32(3 * slot, 1, (8, 64))
        for kc in range(16):
            mm(acck(p=pp), HT(kc, (c0, c0 + nt)), WKV(kc), kc == 0, kc == 15)
        if tn == "s":
            cp("act", VB_S(p=pp), acck((4, 8), p=pp))
            cp("act", VF_S(p=pp), acck((4, 8), p=pp))
        else:
            cp("act", VT(ti, p=pp), acck((4, 8), p=pp))
            if tn == "m7":
                cp("act", VF_L(), acck((4, 8)))
        kf = KF_L() if tn == "m7" else (KF_S(p=pp) if tn == "s" else kb_.C(p=pp))
        kf4 = Ref(kf.ap.rearrange("p (h d) -> p h d", d=64), kf.rngs) if tn in ("m7", "s") else kf
        kout = K2[ti % 3]
        qk_chain(acck, 4, nt, GCK, GSK, ti, kb_, kout(p=pp), out_f32=kf4)
        if len(pend) == 2:
            fin_qk(*pend.pop(0))
        pend.append((kout, 2, KT, ti, c0, nt))
    kpend = pend
    order = [2 * c + j for c in range(8) for j in (1, 0)]
    WSUF = [sb(O_U + 10240 + 4096 * i, BF16, (16, 128)) for i in range(2)]
    for i in range(2):
        dma("pool", WSUF[i](), R(wu_d[order[i]].rearrange("p (k m) -> p k m", m=128)), "wsuf%d" % i)
    pend = []
    qpend = pend
    for ti, (tn, c0, nt) in enumerate(TT):
        if tn == "h":
            continue
        pp = (0, nt)
        slot = next_slot()
        accq = ps_f32(3 * slot, 2, (16, 64))
        for kc in range(16):
            for hf in range(2):
                mm(accq((8 * hf, 8 * hf + 8), p=pp), HT(kc, (c0, c0 + nt)), WQ(kc, (512 * hf, 512 * hf + 512)),
                   kc == 0, kc == 15)
        while kpend:
            fin_qk(*kpend.pop(0))
        qbt = QKBufs()
        qbt.SQ, qbt.B = qb.SQ, qb.B
        qbt.A = qb.A if ti % 2 == 0 else qb.A2
        qout = Q2[ti % 3]
        qk_chain(accq, 16, nt, GCQ, GSQ, ti, qbt, qout(p=pp))
        if len(pend) == 2:
            fin_qk(*pend.pop(0))
        pend.append((qout, 8, QT, ti, c0 - MC0, nt))

    memset("pool", KT((0, 2), (0, 1)), 0.0)
    memset("pool", KT((0, 2), (NCOL - 1, NCOL)), 0.0)
    memset("pool", QT((0, 8), (NM - 1, NM)), 0.0)

    dma("sp", R(nkp), KF_L(), "kvout", group=True, final=True)
    dma("sp", R(nvp), VF_L(), "kvout", group=True, final=True)
    dma("sp", R(nks[:, 127, :]), KF_S(p=(0, SPC)), "kvout", group=True, final=True)
    dma("sp", R(nvs[:, 127, :]), VF_S(p=(0, SPC)), "kvout", group=True, final=True)
    dma("sp", R(nks[:, 0:127, :]), R(ck[:, 1:128, :]), "d2d", group=True, final=True)
    dma("sp", R(nvs[:, 0:127, :]), R(cv[:, 1:128, :]), "d2d", group=True, final=True)
    dma("sp", R(ncs[:, 0:29, :]), R(scv[:, 1:30, :]), "d2d", group=True, final=True)
    if stop == 'P4':
        return end(('qt', QT), ('kt', KT), ('vt', VT))

    WSU = [sb(O_RA + 12384 + 4096 * i, BF16, (16, 128)) for i in range(3)]
    wsi = [0]

    def wload(slots, src_ap, tag, m=128):
        i = wsi[0] % len(slots)
        wsi[0] += 1
        dma("pool", slots[i](), R(src_ap.rearrange("p (k m) -> p k m", m=m)), "%s%d" % (tag, i))
        return slots[i]

    def conv_tap(c, j, col0=0, ncol=NM):
        g0 = MC0 + j - 30 + col0
        uu = U(c, (g0, g0 + ncol))
        cc = CC(c, (col0, col0 + ncol))
        if j == 0:
            P.add("dve", (lambda c=c, uu=uu, cc=cc: (lambda e: e.tensor_scalar(
                out=cc.ap, in0=uu.ap, scalar1=CW(c).ap[:, 0:1], scalar2=CB(c).ap,
                op0=ALU.mult, op1=ALU.add)))(), reads=[uu, CW(c), CB(c)], writes=[cc])
        else:
            P.add("dve", (lambda c=c, uu=uu, cc=cc, j=j: (lambda e: e.scalar_tensor_tensor(
                out=cc.ap, in0=uu.ap, scalar=CW(c).ap[:, j:j + 1], in1=cc.ap,
                op0=ALU.mult, op1=ALU.add)))(), reads=[uu, CW(c), cc], writes=[cc])

    dve_taps = [(c, j) for c in (0, 1) for j in range(31)]
    DGF = sb(O_AT, BF16, (31, 128))
    PU = [(99, 452), (452, 805), (805, 1158)]
    SG = sb(O_TMP, F32, (3, 353))
    order = [2 * c + j for c in range(8) for j in (1, 0)]
    pending = [WSUF[0], WSUF[1]]
    for c in range(8):
        accs = []
        for j in range(2):
            w = pending.pop(0)
            nxt = 2 * c + j + 2
            if nxt < 16:
                pending.append(wload(WSU, wu_d[order[nxt]], "wsu"))
            accb = acc3(next_slot())
            for kc in range(16):
                for pi, (a, b) in enumerate(PU):
                    mm(accb(pi, (0, 353)), w(kc), HT(kc, (a, b)), kc == 0, kc == 15)
            accs.append(accb)
            while qpend:
                fin_qk(*qpend.pop(0))
            if j == 0:
                act(SG(), accb((0, 3), (0, 353)), AF.Sigmoid)
        ur = U(c, (99, NCOL))
        tt("dve", Ref(ur.ap.rearrange("p (a b) -> p a b", b=353), ur.rngs), accs[1]((0, 3), (0, 353)), SG(), ALU.mult)
        tt("dve", UF(c), accs[1](2, (1123 - 805, 1157 - 805)), SG(2, (1123 - 805, 1157 - 805)), ALU.mult)
        if c == 5:
            tt("pool", DGF(), bc(IDB(), 1, (128, 31, 128)), bc(CW(2), 2, (128, 31, 128)), ALU.mult)
        if c >= 1:
            for _ in range(7 if c < 7 else 99):
                if dve_taps:
                    conv_tap(*dve_taps.pop(0))
    if stop == 'P5':
        return end(('u', U), ('uf', UF))

    UTO = sb(O_TMP, F32, (8, 128))
    ptu = ps_f32(6, 2, (8, 128))
    for c in range(8):
        tr(ptu(c, p=(0, 34)), UF(c), IDF())
    cp("act", UTO(p=(0, 34)), ptu(p=(0, 34)))
    dma("sp", R(ncp), Ref(UTO.ap[0:30].rearrange("p a b -> p (a b)"), UTO().rngs), "uto", group=True, final=True)
    dma("sp", R(ncs[:, 29, :]), Ref(UTO.ap[30:34].rearrange("p a b -> p (a b)"), UTO().rngs), "uto", group=True,
        final=True)

    DG = [sb(O_HT + 7936 * i, BF16, (31, 128)) for i in range(2)]
    FS = sb(O_HT + 16384, F32, (8, SPC, 31))
    SCT = sb(O_HT + 20480, F32, (1024,))
    CSM = sb(O_HT + 24576, F32, (8, SPC))
    PRD = sb(O_HT + 24704, F32, (8, SPC, 31))
    assert not dve_taps
    dma("sp", SCT(p=(0, 120)), R(scv.rearrange("s r c -> (s r) c")), "sct", group=True)
    def sample_conv_prep():
        for c in range(8):
            pt = ps_f32(6 + c % 2, 1, (120,))
            tr(pt(), SCT((128 * c, 128 * c + 128), p=(0, 120)), IDF((0, 120), p=(0, 120)))
            pt3 = Buf("ps", pt.ap.rearrange("p (s r) -> p s r", r=30), pt.off, (SPC, 30), 4)
            cp("act", FS(c, (0, SPC), (0, 30)), pt3())
        cp("dve", Ref(FS.ap[:, :, :, 30], FS().rngs), UF((0, 8), (30, 34)))
        cwa = PAR((32, 280))
        cwb = Ref(cwa.ap.rearrange("p (c j) -> p c j", j=31).unsqueeze(2).broadcast_to([128, 8, SPC, 31]), cwa.rngs)
        tt("dve", PRD(), FS(), cwb, ALU.mult)
        red(CSM(), PRD())

    for j in range(31):
        conv_tap(2, j, 0, 172)
    for c in range(2, 8):
        if c == 2:
            dg = DGF
        else:
            dg = DG[c % 2]
            tt("pool", dg(), bc(IDB(), 1, (128, 31, 128)), bc(CW(c), 2, (128, 31, 128)), ALU.mult)
        accb = acc3(next_slot())
        for pi, (a, b) in enumerate(PM):
            a0 = 172 if (c == 2 and pi == 0) else a
            n_ = b - a0
            for j in range(31):
                g0 = MC0 + a0 + j - 30
                mm(accb(pi, (0, n_)), dg(j), U(c, (g0, g0 + n_)), j == 0, j == 30)
        c3 = sub3(CC, c, 343)
        if c == 2:
            act(CC(c, (172, 343)), accb(0, (0, 171)), AF.Identity, bias=CB(c))
            act(c3((1, 3)), accb((1, 3), (0, 343)), AF.Identity, bias=CB(c))
            sample_conv_prep()
        else:
            act(c3(), accb((0, 3), (0, 343)), AF.Identity, bias=CB(c))
    tt("dve", CC((0, 8), (1024, 1024 + SPC)), CSM(), bc(PAR((280, 288)), 2, (128, 8, SPC)), ALU.add)
    if stop == 'P7':
        return end(('cc', CC))

    LB16 = [sb(O_HT + 28672 + 2080 * i, BF16, (NM,)) for i in range(2)]
    LSQ = [sb(O_HT + 32832 + 2080 * i, BF16, (NM,)) for i in range(2)]
    accm = acc3(next_slot())
    accv = acc3(next_slot())
    for c in range(8):
        cb16 = LB16[c % 2]
        csq = LSQ[c % 2]
        cp("dve", cb16(), CC(c))
        if c % 3 == 2:
            tt("dve", csq(), CC(c), CC(c), ALU.mult)
        else:
            act(csq(), CC(c), AF.Square)
        for pi, (a, b) in enumerate(PM):
            mm(accm(pi, (0, 343)), ONM(), cb16((a, b)), c == 0, c == 7)
        for pi, (a, b) in enumerate(PM):
            mm(accv(pi, (0, 343)), ONM(), csq((a, b)), c == 0, c == 7)
    MEAN = sb(O_HT, F32, (3, 343))
    RSL = sb(O_HT + 4128, F32, (3, 343))
    SGL = [sb(O_HT + 8256 + 2080 * i, BF16, (3, 343)) for i in range(2)]
    cp("dve", MEAN(), accm((0, 3), (0, 343)))
    act(RSL(), accm((0, 3), (0, 343)), AF.Square)
    tt("dve", RSL(), accv((0, 3), (0, 343)), RSL(), ALU.subtract)
    P.add("dve", lambda e: e.tensor_scalar(out=RSL().ap, in0=RSL().ap, scalar1=0.0, scalar2=None, op0=ALU.max),
          reads=[RSL()], writes=[RSL()])
    act(RSL(), RSL(), AF.Ln, bias=EPS)
    act(RSL(), RSL(), AF.Exp, scale=-0.5)

    def ln_chunk_a(c):
        c3 = sub3(CC, c, 343)
        tt("pool", c3(), c3(), MEAN(), ALU.subtract)
        tt("pool", c3(), c3(), RSL(), ALU.mult)
        P.add("pool", (lambda c3=c3, c=c: (lambda e: e.tensor_scalar(
            out=c3().ap, in0=c3().ap, scalar1=LG(c).ap, scalar2=LB(c).ap, op0=ALU.mult, op1=ALU.add)))(),
            reads=[c3(), LG(c), LB(c)], writes=[c3()])

    def ln_chunk_b(c):
        c3 = sub3(CC, c, 343)
        sg = SGL[c % 2]
        act(sg(), c3(), AF.Sigmoid)
        tt("dve", sub3(CT, c, 343)(), c3(), sg(), ALU.mult)

    PT = [sb(O_TMP + 1024 * i, BF16, (4, 128)) for i in range(8)]
    REC = [sb(O_HT + 16384 + 2048 * i, F32, (4, 128)) for i in range(2)]
    units = [(n, gp) for n in range(8) for gp in range(2)]

    def unit_a(ui):
        n, gp = units[ui]
        own_c0 = MC0 + 128 * n
        cs = (gp * 4, gp * 4 + 4)
        pso = ps_f32(4 + ui % 2, 1, (4, 128))
        psd = ps_f32(6 + ui % 2, 1, (4, 128))
        pts = {}
        for kb, kc0 in enumerate((own_c0 - 128, own_c0)):
            for gi in range(2):
                pr = (64 * gi, 64 * gi + 64)
                pss = ps_f32(2 * gi + kb, 1, (4, 128))
                mb = MBOWN if kb == 1 else (MBPR0 if n == 0 else MBPRV)
                mm(pss(), KT(gp, (kc0, kc0 + 128), p=pr), QT(cs, (own_c0 - MC0, own_c0 - MC0 + 128), p=pr),
                   True, False)
                mm(pss(), IDB(), bc(mb(), 1, (128, 4, 128)), False, True)
                pt = PT[(ui % 2) * 4 + 2 * gi + kb]
                act(pt(), pss(), AF.Exp, scale=0.125)
                pts[(gi, kb)] = pt
        for kb in range(2):
            for gi in range(2):
                pr = (64 * gi, 64 * gi + 64)
                g = 2 * gp + gi
                mm(pso(p=pr), VT(n + kb, (64 * g, 64 * g + 64)), pts[(gi, kb)](), kb == 0, kb == 1)
            for gi in range(2):
                pr = (64 * gi, 64 * gi + 64)
                mm(psd(p=pr), ONB((0, 64)), pts[(gi, kb)](), kb == 0, kb == 1)

    def unit_b(ui):
        n, gp = units[ui]
        own_c0 = MC0 + 128 * n
        cs = (gp * 4, gp * 4 + 4)
        pso = ps_f32(4 + ui % 2, 1, (4, 128))
        psd = ps_f32(6 + ui % 2, 1, (4, 128))
        rec = REC[ui % 2]
        tt("dve", rec(), psd(), bc(ESK(cs), 2, (128, 4, 128)), ALU.add)
        act(rec(), rec(), AF.Ln)
        act(rec(), rec(), AF.Exp, scale=-1.0)
        tt("dve", AT(cs, (own_c0 - MC0, own_c0 - MC0 + 128)), pso(), rec(), ALU.mult)

    if stop == 'P8a':
        return end(('cc', CC))
    CKB = sb(O_HT + 20480, BF16, (SPC, 256))
    CVB = sb(O_HT + 22528, BF16, (SPC, 256))
    KTS = sb(O_HT + 24576, BF16, (SPC, 2, 128))
    PTS = sb(O_HT + 26624, BF16, (64,))
    RECS = sb(O_HT + 26752, F32, (8, SPC))
    dma("pool", CKB(p=(0, 127)), R(ck[:, 1:128, :].rearrange("s k d -> k s d")), "cks", group=True)
    dma("pool", CVB(p=(0, 127)), R(cv[:, 1:128, :].rearrange("s k d -> k s d")), "cks", group=True)
    for s in range(SPC):
        dma("sp", CVB(s, p=(127, 128)), VB_S(p=(s, s + 1)), "cvs", group=True)
    unit_a(0)
    for ui in range(16):
        if ui + 1 < 16:
            unit_a(ui + 1)
        unit_b(ui)
        if ui % 2 == 1:
            ln_chunk_a(ui // 2)
    ptk = ps_bf16(0, (8, 128))
    for s in range(SPC):
        for c in range(2):
            tr(ptk(2 * s + c, (0, 127)), CKB(s, (128 * c, 128 * c + 128), p=(0, 127)), IDB((0, 127), p=(0, 127)))
    ptk3 = Buf("ps", ptk.ap.rearrange("p (s c) k -> p s c k", c=2), ptk.off, (SPC, 2, 128), 2)
    cp("act", KTS((0, SPC), (0, 2), (0, 127)), ptk3((0, SPC), (0, 2), (0, 127)))
    ktn = KT((0, 2), (SC0, SC0 + SPC))
    cp("dve", Ref(KTS.ap[:, :, :, 127], KTS().rngs), Ref(ktn.ap.rearrange("p c s -> p s c"), ktn.rngs))
    pss2 = [ps_f32(1, 1, (64,)), ps_f32(4, 1, (64,))]
    for s in range(SPC):
        for g in range(4):
            pr = (64 * (g % 2), 64 * (g % 2) + 64)
            cs = ((g // 2) * 4, (g // 2) * 4 + 4)
            mm(pss2[g % 2]((16 * s + 4 * g, 16 * s + 4 * g + 4)), KTS(s, g // 2, p=pr),
               QT(cs, (1024 + s, 1025 + s), p=pr), True, True)
    PTS4 = Buf("sb", PTS.ap.rearrange("p (a g h) -> p a g h", g=2, h=4), PTS.off, (8, 2, 4), 2)
    for hf in range(2):
        src_ = pss2[hf]()
        src4 = Ref(src_.ap.rearrange("p (a g h) -> p a g h", g=2, h=4)[:, :, hf, :], src_.rngs)
        dst_ = PTS()
        act(Ref(PTS4.ap[:, :, hf, :], dst_.rngs), src4, AF.Exp, scale=0.125)
    psos = ps_f32(2, 1, (8, SPC))
    psds = ps_f32(3, 1, (8, SPC))
    for s in range(SPC):
        for g in range(4):
            pr = (64 * (g % 2), 64 * (g % 2) + 64)
            cs = ((g // 2) * 4, (g // 2) * 4 + 4)
            rhs = PTS((16 * s + 4 * g, 16 * s + 4 * g + 4))
            mm(psos(cs, (s, s + 1), p=pr), CVB(s, (64 * g, 64 * g + 64)), rhs, True, True)
            mm(psds(cs, (s, s + 1), p=pr), ONB((0, 64)), rhs, True, True)
    tt("dve", RECS(), psds(), bc(ESK(), 2, (128, 8, SPC)), ALU.add)
    act(RECS(), RECS(), AF.Ln)
    act(RECS(), RECS(), AF.Exp, scale=-1.0)
    tt("dve", AT((0, 8), (1024, 1024 + SPC)), psos(), RECS(), ALU.mult)
    memset("pool", AT((0, 8), (NM - 1, NM)), 0.0)
    for c in range(8):
        ln_chunk_b(c)
    if stop == 'P8':
        return end(('ct', CT))
    if stop == 'P6a':
        return end(('at', AT))

    if stop == 'P6':
        return end(('at', AT))

    WSO = [sb(O_QT + 4096 * i, BF16, (16, 128)) for i in range(3)]
    SQ2 = [sb(O_TMP + 4128 + i * 2080, BF16, (NM,)) for i in range(2)]
    stm = ps_f32(6, 2, (2, 512))

    def norm2_stats(dc):
        sq = SQ2[dc % 2]
        if dc % 4 != 3:
            act(sq((0, 1024)), XT(dc, (0, 1024)), AF.Square)
        else:
            tt("dve", sq((0, 1024)), XT(dc, (0, 1024)), XT(dc, (0, 1024)), ALU.mult)
        for hfc in range(2):
            mm(stm(hfc), ONB(), sq((512 * hfc, 512 * hfc + 512)), dc == 0, dc == 15)

    pending = [wload(WSO, wo_d[0], "wso"), wload(WSO, wo_d[1], "wso")]
    for dc in range(16):
        w = pending.pop(0)
        if dc + 2 < 16:
            pending.append(wload(WSO, wo_d[dc + 2], "wso"))
        accb = acc3(next_slot())
        for kc in range(16):
            src = AT if kc < 8 else CT
            for pi, (a, b) in enumerate(PM):
                mm(accb(pi, (0, 343)), w(kc), src(kc % 8, (a, b)), kc == 0, kc == 15)
        x3 = sub3(XT, dc, 343)
        tt("dve", x3(), accb((0, 3), (0, 343)), x3(), ALU.add)
        if dc >= 1:
            norm2_stats(dc - 1)
            act(HM(dc - 1), XT(dc - 1), AF.Identity, scale=G2(dc - 1))
    norm2_stats(15)
    act(HM(15), XT(15), AF.Identity, scale=G2(15))
    if stop == 'P9':
        return end(('xt', XT))

    SQS = sb(O_TMP + 4128, BF16, (16, 5))
    sts = ps_f32(0, 1, (5,))
    act(SQS(), XT((0, 16), (1024, NM)), AF.Square)
    for c in range(16):
        mm(sts(), ONB(), SQS(c), c == 0, c == 15)
    RS2f = sb(O_TMP, F32, (NM,))
    act(RS2f((0, 1024)), Ref(stm().ap.rearrange("p a b -> p (a b)"), stm().rngs), AF.Ln, bias=EPS, scale=1.0 / D)
    act(RS2f((1024, NM)), sts(), AF.Ln, bias=EPS, scale=1.0 / D)
    act(RS2f(), RS2f(), AF.Exp, scale=-1.0)
    if stop == 'P10':
        return end(('hm', HM))
    HID = [sb(O_RA + i * 8256, BF16, (GFF, NM)) for i in range(2)]
    WUP = [sb(O_QT + 12288 + 4096 * i, BF16, (16, 128)) for i in range(3)]
    WDN = [sb(O_QT + 24576 + 4096 * i, BF16, (16, 128)) for i in range(2 * GFF)]
    assert O_QT + 24576 + 4096 * 2 * GFF <= O_TMP
    YSB = [sb(O_RA + 16512 + 4608 * i, F32, (9, 128)) for i in range(3)]
    NG = NFF // GFF
    wupi = [0]

    def wup_load(f):
        i = wupi[0] % 3
        wupi[0] += 1
        dma("pool", WUP[i](), R(wup_d[f].rearrange("p (k m) -> p k m", m=128)), "wup%d" % i)
        return WUP[i]

    def wdn_load(f):
        i = f % (2 * GFF)
        dma("pool", WDN[i](), R(wdn_d[f].rearrange("p (k m) -> p k m", m=128)), "wdn%d" % i)
        return WDN[i]

    upq = [wup_load(0), wup_load(1)]
    nextup = [2]

    def up_group(g):
        hid = HID[g % 2]
        for fi in range(GFF):
            w = upq.pop(0)
            if nextup[0] < NFF:
                upq.append(wup_load(nextup[0]))
                nextup[0] += 1
            accb = acc3(next_slot())
            for kc in range(16):
                for pi, (a, b) in enumerate(PM):
                    mm(accb(pi, (0, 343)), w(kc), HM(kc, (a, b)), kc == 0, kc == 15)
            act(sub3(hid, fi, 343)(), accb((0, 3), (0, 343)), AF.Relu)
            tt("dve", hid(fi), hid(fi), hid(fi), ALU.mult)
            tt("dve", hid(fi), hid(fi), RS2f(), ALU.mult)

    def out_chunk(dc):
        ysb = YSB[dc % 3]
        for q4 in range(2):
            pt = ps_f32(6 + q4, 1, (4, 128))
            for j in range(4):
                t = 4 * q4 + j
                tr(pt(j), XT(dc, (128 * t, 128 * t + 128)), IDF())
            cp("act", ysb((4 * q4, 4 * q4 + 4)), pt())
        pt = ps_f32(6, 1, (4, 128))
        tr(pt(0, p=(0, SPC)), XT(dc, (1024, 1024 + SPC)), IDF())
        cp("act", ysb(8, p=(0, SPC)), pt(0, p=(0, SPC)))
        dma("sp", R(yp.rearrange("(t p) d -> p t d", p=128)[:, :, 128 * dc:128 * dc + 128]), ysb((0, 8)),
            "ys%d" % (dc % 3), final=True)
        dma("sp", R(ys[:, 128 * dc:128 * dc + 128]), ysb(8, p=(0, SPC)), "ysb%d" % (dc % 3), final=True)

    def down_group(g, wd, last):
        hid = HID[g % 2]
        for dc in range(16):
            accb = acc3(next_slot())
            for fi in range(GFF):
                for pi, (a, b) in enumerate(PM):
                    mm(accb(pi, (0, 343)), wd[fi](dc), hid(fi, (a, b)), fi == 0, fi == GFF - 1)
            x3 = sub3(XT, dc, 343)
            tt("dve", x3(), accb((0, 3), (0, 343)), x3(), ALU.add)
            if last and dc >= 1:
                out_chunk(dc - 1)
        if last:
            out_chunk(15)

    up_group(0)
    wdq = {0: [wdn_load(f) for f in range(GFF)]}
    for g in range(NG):
        if g + 1 < NG:
            up_group(g + 1)
            wdq[g + 1] = [wdn_load((g + 1) * GFF + fi) for fi in range(GFF)]
        down_group(g, wdq.pop(g), g == NG - 1)

    return end()


def _host_weights(w_in, w_out, w_up, w_down):
    w_in = np.asarray(w_in[0], dtype=np.float32)
    w_out = np.asarray(w_out[0], dtype=np.float32)
    w_up = np.asarray(w_up[0], dtype=np.float32)
    w_down = np.asarray(w_down[0], dtype=np.float32)
    qcols = np.concatenate([np.arange(64) + 64 * head_of(c, hf) for c in range(8) for hf in range(2)])
    wq = np.ascontiguousarray(w_in[:, qcols].reshape(16, 128, 1024))
    wkv = np.ascontiguousarray(w_in[:, 1024:1536].reshape(16, 128, 512))
    wu = np.empty((16, 128, 16, 128), np.float32)
    for c in range(8):
        for j, basec in enumerate((1536, 2560)):
            blk = w_in[:, basec + 128 * c: basec + 128 * c + 128]
            wu[2 * c + j] = blk.reshape(16, 128, 128).transpose(1, 0, 2)
    wu = wu.reshape(16, 128, 2048)
    rows = np.concatenate([np.arange(64) + 64 * head_of(c, hf) for c in range(8) for hf in range(2)]
                          + [np.arange(1024, 2048)])
    wo_p = w_out[rows]
    wo = np.ascontiguousarray(wo_p.reshape(16, 128, 16, 128).transpose(2, 1, 0, 3)).reshape(16, 128, 2048)
    wup = np.ascontiguousarray(w_up.reshape(16, 128, NFF, 128).transpose(2, 1, 0, 3)).reshape(NFF, 128, 2048)
    wdn = np.ascontiguousarray(w_down.reshape(NFF, 128, 2048))
    return wq, wkv, wu, wo, wup, wdn


def _host_par(norm_mix_g, q_norm_g, k_norm_g, sinks, conv_w, conv_b, conv_ln_g, conv_ln_b, norm_mlp_g):
    par = np.zeros((128, 640), np.float32)
    par[:, 0:16] = np.asarray(norm_mix_g[0]).reshape(16, 128).T
    par[:, 16:32] = np.asarray(norm_mlp_g[0]).reshape(16, 128).T
    cw = np.asarray(conv_w[0])
    par[:, 32:280] = cw.reshape(31, 8, 128).transpose(2, 1, 0).reshape(128, 248)
    par[:, 280:288] = np.asarray(conv_b[0]).reshape(8, 128).T
    par[:, 288:296] = np.asarray(conv_ln_g[0]).reshape(8, 128).T
    par[:, 296:304] = np.asarray(conv_ln_b[0]).reshape(8, 128).T
    sk = np.asarray(sinks[0])
    for c in range(8):
        par[0:64, 304 + c] = sk[head_of(c, 0)]
        par[64:128, 304 + c] = sk[head_of(c, 1)]
    gq = np.asarray(q_norm_g[0])
    gk = np.asarray(k_norm_g[0])
    par[:, 320:384] = gq[None, :]
    par[:, 384:448] = gk[None, :]
    par[:, 448:512] = np.concatenate([gq[32:], gq[:32]])[None, :]
    par[:, 512:576] = np.concatenate([gk[32:], gk[:32]])[None, :]
    return par


def _host_tab(half):
    inv = (10000.0 ** (-np.arange(32, dtype=np.float32) / np.float32(32))).astype(np.float32)
    tab = np.zeros((128, 2, 10, 64), np.float32)
    for ti in range(10):
        if ti == 0:
            pos = half * 1024 - 128 + np.arange(128)
            pos = np.maximum(pos, 0)
        elif ti == 9:
            pos = np.full(128, PAST)
        else:
            pos = half * 1024 + (ti - 1) * 128 + np.arange(128)
        ang = pos.astype(np.float32)[:, None] * inv[None, :]
        ang = ang.astype(np.float64)
        c = np.cos(ang).astype(np.float32)
        s = np.sin(ang).astype(np.float32)
        tab[:, 0, ti, :32] = c
        tab[:, 0, ti, 32:] = c
        tab[:, 1, ti, :32] = -s
        tab[:, 1, ti, 32:] = s
    return tab.reshape(128, 1280)


def _host_cst(half):
    NEGB = -30000.0
    cst = np.zeros((128, 512), np.float32)
    cst[:, 0:128] = np.eye(128, dtype=np.float32)
    k = np.arange(128)[:, None]
    q = np.arange(128)[None, :]
    cst[:, 128:256] = np.where(k <= q, 0.0, NEGB)
    cst[:, 256:384] = np.where(k > q, 0.0, NEGB)
    cst[:, 384:512] = np.where(k > q, 0.0, NEGB) if half == 1 else NEGB
    return cst


_NC_CACHE = {}


def kernel(x_prompt, x_sample, cache_k, cache_v, state_conv, norm_mix_g, w_in, q_norm_g, k_norm_g, sinks,
           conv_w, conv_b, conv_ln_g, conv_ln_b, w_out, norm_mlp_g, w_up, w_down):
    x_prompt = np.asarray(x_prompt, np.float32)
    x_sample = np.asarray(x_sample, np.float32)
    cache_k = np.asarray(cache_k, np.float32)
    cache_v = np.asarray(cache_v, np.float32)
    state_conv = np.asarray(state_conv, np.float32)
    wq, wkv, wu, wo, wup, wdn = _host_weights(w_in, w_out, w_up, w_down)
    par = _host_par(norm_mix_g, q_norm_g, k_norm_g, sinks, conv_w, conv_b, conv_ln_g, conv_ln_b, norm_mlp_g)
    if "nc" not in _NC_CACHE:
        _NC_CACHE["nc"] = build_program()
    nc = _NC_CACHE["nc"]
    in_maps = []
    for c in range(NCORE):
        b, half = c // 2, c % 2
        xm = np.ascontiguousarray(x_prompt[b, half * 1024:(half + 1) * 1024])
        if half == 1:
            xh = np.ascontiguousarray(x_prompt[b, 896:1024])
        else:
            xh = np.zeros((128, D), np.float32)
        in_maps.append({
            "xm": xm, "xh": xh,
            "xs": np.ascontiguousarray(x_sample[SPC * c:SPC * c + SPC, 0]),
            "ck": np.ascontiguousarray(cache_k[0, SPC * c:SPC * c + SPC].reshape(SPC, 128, 256)),
            "cv": np.ascontiguousarray(cache_v[0, SPC * c:SPC * c + SPC].reshape(SPC, 128, 256)),
            "sc": np.ascontiguousarray(state_conv[0, SPC * c:SPC * c + SPC]),
            "wq": wq, "wkv": wkv, "wu": wu, "wo": wo, "wup": wup, "wdn": wdn,
            "par": par, "tab": _host_tab(half), "cst": _host_cst(half),
        })
    res = run_bass_kernel_spmd(nc, in_maps, core_ids=list(range(NCORE)))
    r = res.results
    y_prompt = np.empty((NB, SEQ, D), np.float32)
    y_sample = np.empty((NS, 1, D), np.float32)
    nkp = np.empty((1, NB, 128, 4, 64), np.float32)
    nvp = np.empty((1, NB, 128, 4, 64), np.float32)
    ncp = np.empty((1, NB, 30, 1024), np.float32)
    nks = np.empty((1, NS, 128, 4, 64), np.float32)
    nvs = np.empty((1, NS, 128, 4, 64), np.float32)
    ncs = np.empty((1, NS, 30, 1024), np.float32)
    for c in range(NCORE):
        b, half = c // 2, c % 2
        y_prompt[b, half * 1024:(half + 1) * 1024] = r[c]["yp"]
        y_sample[SPC * c:SPC * c + SPC, 0] = r[c]["ys"]
        if half == 1:
            nkp[0, b] = r[c]["nkp"].reshape(128, 4, 64)
            nvp[0, b] = r[c]["nvp"].reshape(128, 4, 64)
            ncp[0, b] = r[c]["ncp"]
        nks[0, SPC * c:SPC * c + SPC] = r[c]["nks"].reshape(SPC, 128, 4, 64)
        nvs[0, SPC * c:SPC * c + SPC] = r[c]["nvs"].reshape(SPC, 128, 4, 64)
        ncs[0, SPC * c:SPC * c + SPC] = r[c]["ncs"]
    return (y_prompt, y_sample, nkp, nvp, ncp, nks, nvs, ncs)
```
